# Optimizing a Trainium2 kernel written in Bass

```python
import jax, jax.numpy as jnp
from jax import lax
import numpy as np

D_MODEL = 1024
BATCH = 4
SEQ = 8192
DEPTH = 1

SSM_HEADDIM = 64
SSM_HEADS = 16
SSM_INNER = SSM_HEADS * SSM_HEADDIM
SSM_GROUPS = 2
SSM_HPG = SSM_HEADS // SSM_GROUPS
SSM_STATE = 64
CONV_K = 4
CHUNK = 128
CONV_DIM = SSM_INNER + 2 * SSM_GROUPS * SSM_STATE

ATT_HEADS = 8
ATT_HEADDIM = 128
ATT_INNER = ATT_HEADS * ATT_HEADDIM
Q_BLOCK = 128

N_BRANCHES = 2
EPS = 1e-6

IN_SPLITS = (SSM_INNER, CONV_DIM, SSM_HEADS, ATT_INNER, ATT_INNER, ATT_INNER, ATT_INNER, ATT_HEADS, N_BRANCHES * D_MODEL)
IN_COLS = int(sum(IN_SPLITS))

kernel_name = "hybrid_ssd_fox_gated_block"


def _rmsnorm(x, w):
    xf = x.astype(jnp.float32)
    y = xf * lax.rsqrt(jnp.mean(xf * xf, axis=-1, keepdims=True) + EPS)
    return (y * w.astype(jnp.float32)).astype(x.dtype)


def _causal_depthwise_conv(u, w, b):
    ch = u.shape[-1]
    out = lax.conv_general_dilated(
        u, w[:, None, :].astype(u.dtype), window_strides=(1,), padding=[(CONV_K - 1, 0)],
        dimension_numbers=("NWC", "WIO", "NWC"), feature_group_count=ch)
    return out + b.astype(u.dtype)


def _ssd_branch(z, xbc, dt_raw, conv_w, conv_b, dt_bias, a_log, d_skip, ssm_norm_w):
    bsz, s, _ = xbc.shape
    g, r, p, n = SSM_GROUPS, SSM_HPG, SSM_HEADDIM, SSM_STATE
    nc = s // CHUNK
    xbc = jax.nn.silu(_causal_depthwise_conv(xbc, conv_w, conv_b))
    xs, bm, cm = jnp.split(xbc, [SSM_INNER, SSM_INNER + g * n], axis=-1)
    xs = xs.reshape(bsz, s, g, r, p)
    dt = jax.nn.softplus(dt_raw.astype(jnp.float32) + dt_bias.astype(jnp.float32)).reshape(bsz, s, g, r)
    a = dt * (-jnp.exp(a_log.astype(jnp.float32))).reshape(g, r)
    xdt = xs.astype(jnp.float32) * dt[..., None]
    xc = xdt.reshape(bsz, nc, CHUNK, g, r, p)
    bc = bm.astype(jnp.float32).reshape(bsz, nc, CHUNK, g, n)
    cc = cm.astype(jnp.float32).reshape(bsz, nc, CHUNK, g, n)
    ac = a.reshape(bsz, nc, CHUNK, g, r).transpose(0, 3, 4, 1, 2)
    a_cs = jnp.cumsum(ac, axis=-1)
    tril = jnp.tril(jnp.ones((CHUNK, CHUNK), dtype=bool))
    seg = a_cs[..., :, None] - a_cs[..., None, :]
    lmat = jnp.exp(jnp.where(tril, seg, -jnp.inf))
    cb = jnp.einsum("bclgn,bcsgn->bgcls", cc, bc)
    y_diag = jnp.einsum("bgrcls,bcsgrp->bclgrp", cb[:, :, None] * lmat, xc)
    decay_states = jnp.exp(a_cs[..., -1:] - a_cs)
    states = jnp.einsum("bclgn,bgrcl,bclgrp->bcgrpn", bc, decay_states, xc)
    states = jnp.concatenate([jnp.zeros_like(states[:, :1]), states], axis=1)
    cs = jnp.cumsum(jnp.pad(a_cs[..., -1], ((0, 0), (0, 0), (0, 0), (1, 0))), axis=-1)
    tril_c = jnp.tril(jnp.ones((nc + 1, nc + 1), dtype=bool))
    decay_chunk = jnp.exp(jnp.where(tril_c, cs[..., :, None] - cs[..., None, :], -jnp.inf))
    new_states = jnp.einsum("bgrzc,bcgrpn->bzgrpn", decay_chunk, states)
    prev_states = new_states[:, :-1]
    y_off = jnp.einsum("bclgn,bcgrpn,bgrcl->bclgrp", cc, prev_states, jnp.exp(a_cs))
    y = (y_diag + y_off).reshape(bsz, s, g, r, p) + d_skip.astype(jnp.float32).reshape(g, r)[..., None] * xs.astype(jnp.float32)
    y = y.reshape(bsz, s, SSM_INNER).astype(z.dtype)
    return _rmsnorm(y * jax.nn.silu(z), ssm_norm_w)


def _fox_branch(q, k, v, z, f_logit, b_f):
    bsz, s, _ = q.shape
    nb = s // Q_BLOCK
    q = q.reshape(bsz, s, ATT_HEADS, ATT_HEADDIM).transpose(0, 2, 1, 3)
    k = k.reshape(bsz, s, ATT_HEADS, ATT_HEADDIM).transpose(0, 2, 1, 3)
    v = v.reshape(bsz, s, ATT_HEADS, ATT_HEADDIM).transpose(0, 2, 1, 3)
    log_f = jax.nn.log_sigmoid(f_logit.astype(jnp.float32) + b_f.astype(jnp.float32))
    f_cum = jnp.cumsum(log_f, axis=1).transpose(0, 2, 1)
    scale = ATT_HEADDIM ** -0.5
    qb = q.reshape(bsz, ATT_HEADS, nb, Q_BLOCK, ATT_HEADDIM).transpose(2, 0, 1, 3, 4)
    fq = f_cum.reshape(bsz, ATT_HEADS, nb, Q_BLOCK).transpose(2, 0, 1, 3)
    qpos = jnp.arange(s, dtype=jnp.int32).reshape(nb, Q_BLOCK)
    kpos = jnp.arange(s, dtype=jnp.int32)

    def one_block(args):
        qi, fi, pi = args
        logits = jnp.einsum("bhqd,bhkd->bhqk", qi, k).astype(jnp.float32) * scale
        logits = logits + fi[..., :, None] - f_cum[..., None, :]
        logits = jnp.where(pi[:, None] >= kpos[None, :], logits, -jnp.inf)
        probs = jax.nn.softmax(logits, axis=-1).astype(v.dtype)
        return jnp.einsum("bhqk,bhkd->bhqd", probs, v)

    o = lax.map(one_block, (qb, fq, qpos))
    o = o.transpose(1, 0, 3, 2, 4).reshape(bsz, s, ATT_INNER)
    return o * jax.nn.silu(z)


def setup_inputs(seed: int = 0) -> dict:
    key = jax.random.key(seed)
    ks = jax.random.split(key, 20)
    d = D_MODEL
    x = jax.random.normal(ks[0], (BATCH, SEQ, d), jnp.float32)
    c = jax.random.normal(ks[1], (BATCH, d), jnp.float32)
    w_ada = jax.random.normal(ks[2], (d, 3 * d), jnp.float32) * d ** -0.5 * 0.5
    b_ada = jax.random.normal(ks[3], (3 * d,), jnp.float32) * 0.01
    norm_w = 1.0 + 0.05 * jax.random.normal(ks[4], (d,), jnp.float32)
    w_in = jax.random.normal(ks[5], (d, IN_COLS), jnp.float32) * d ** -0.5
    conv_w = jax.random.normal(ks[6], (CONV_K, CONV_DIM), jnp.float32) * 0.5
    conv_b = jax.random.normal(ks[7], (CONV_DIM,), jnp.float32) * 0.02
    dt0 = jnp.exp(jax.random.uniform(ks[8], (SSM_HEADS,), jnp.float32, np.log(1e-3), np.log(1e-1)))
    dt_bias = dt0 + jnp.log(-jnp.expm1(-dt0))
    a_log = jnp.log(jax.random.uniform(ks[9], (SSM_HEADS,), jnp.float32, 1.0, 16.0))
    d_skip = 1.0 + 0.1 * jax.random.normal(ks[10], (SSM_HEADS,), jnp.float32)
    ssm_norm_w = 1.0 + 0.05 * jax.random.normal(ks[11], (SSM_INNER,), jnp.float32)
    b_f = jnp.linspace(1.0, 6.0, ATT_HEADS, dtype=jnp.float32) + 0.1 * jax.random.normal(ks[12], (ATT_HEADS,), jnp.float32)
    b_gate = jax.random.normal(ks[13], (N_BRANCHES * d,), jnp.float32) * 0.01
    w_proj_ssm = jax.random.normal(ks[14], (SSM_INNER, d), jnp.float32) * SSM_INNER ** -0.5
    w_proj_att = jax.random.normal(ks[15], (ATT_INNER, d), jnp.float32) * ATT_INNER ** -0.5
    w_out = jax.random.normal(ks[16], (d, d), jnp.float32) * d ** -0.5
    final_norm_w = 1.0 + 0.05 * jax.random.normal(ks[17], (d,), jnp.float32)
    return {"x": x, "c": c, "w_ada": w_ada, "b_ada": b_ada, "norm_w": norm_w, "w_in": w_in,
            "conv_w": conv_w, "conv_b": conv_b, "dt_bias": dt_bias, "a_log": a_log, "d_skip": d_skip,
            "ssm_norm_w": ssm_norm_w, "b_f": b_f, "b_gate": b_gate, "w_proj_ssm": w_proj_ssm,
            "w_proj_att": w_proj_att, "w_out": w_out, "final_norm_w": final_norm_w}


def reference(x, c, w_ada, b_ada, norm_w, w_in, conv_w, conv_b, dt_bias, a_log, d_skip,
              ssm_norm_w, b_f, b_gate, w_proj_ssm, w_proj_att, w_out, final_norm_w):
    ada = jax.nn.silu(c) @ w_ada + b_ada
    shift, scale, gate = jnp.split(ada, 3, axis=-1)
    offsets = [int(o) for o in np.cumsum(IN_SPLITS)[:-1]]
    for _ in range(DEPTH):
        h = _rmsnorm(x, norm_w) * (1.0 + scale[:, None, :]) + shift[:, None, :]
        proj = jnp.einsum("bsd,de->bse", h, w_in)
        z_ssm, xbc, dt_raw, q, k, v, z_att, f_logit, g_logit = jnp.split(proj, offsets, axis=-1)
        y_ssm = _ssd_branch(z_ssm, xbc, dt_raw, conv_w, conv_b, dt_bias, a_log, d_skip, ssm_norm_w)
        y_att = _fox_branch(q, k, v, z_att, f_logit, b_f)
        g_ssm, g_att = jnp.split(jax.nn.sigmoid(g_logit + b_gate), N_BRANCHES, axis=-1)
        merged = g_ssm * jnp.einsum("bse,ed->bsd", y_ssm, w_proj_ssm) + g_att * jnp.einsum("bse,ed->bsd", y_att, w_proj_att)
        x = x + gate[:, None, :] * jnp.einsum("bsd,de->bse", merged, w_out)
    return _rmsnorm(x, final_norm_w)
```

```python
import numpy as np
import concourse.bass as bass
import concourse.mybir as mybir
from concourse.bass_utils import run_bass_kernel_spmd

F32 = mybir.dt.float32
BF16 = mybir.dt.bfloat16
AF = mybir.ActivationFunctionType
ALU = mybir.AluOpType
AX = mybir.AxisListType


MUTE = [False]


class Res:
    __slots__ = ("w", "r", "name")

    def __init__(self, name=""):
        self.w = None
        self.r = {}
        self.name = name


class Op:
    __slots__ = ("eng", "fn", "deps", "idx", "dma_sem", "dma_cnt", "need_inc", "inc_cnt")


class Sched:
    ENGS = ("pe", "act", "dve", "pool", "sp")

    def __init__(self, nc):
        self.nc = nc
        self.ops = []
        self.per_eng = {e: [] for e in self.ENGS}
        self.dma_sems = {}
        self.bar = {}
        self.last = {}

    def _dep(self, deps, op):
        if op is None:
            return
        if op.dma_sem is not None:
            k = ("d", op.dma_sem)
            if deps.get(k, 0) < op.dma_cnt:
                deps[k] = op.dma_cnt
        else:
            k = ("e", op.eng)
            if k not in deps or deps[k].idx < op.idx:
                deps[k] = op

    def add(self, eng, fn, reads=(), writes=(), dma=None):
        if MUTE[0]:
            return None
        op = Op()
        op.eng = eng
        op.fn = fn
        op.idx = len(self.per_eng[eng])
        op.need_inc = False
        op.inc_cnt = 0
        op.dma_sem = None
        op.dma_cnt = 0
        deps = dict(self.bar)
        for r in reads:
            self._dep(deps, r.w)
        for w in writes:
            self._dep(deps, w.w)
            for ro in w.r.values():
                self._dep(deps, ro)
        if dma is not None:
            ent = self.dma_sems.setdefault(dma, [None, 0])
            ent[1] += 16
            op.dma_sem = dma
            op.dma_cnt = ent[1]
        if eng == "pe":
            deps.pop(("e", "pe"), None)
        op.deps = deps
        for k, d in deps.items():
            if k[0] == "e":
                d.need_inc = True
        rk = ("d", dma) if dma is not None else eng
        for r in reads:
            r.r[rk] = op
        for w in writes:
            w.w = op
            w.r = {}
        self.per_eng[eng].append(op)
        self.ops.append(op)
        if dma is None:
            self.last[eng] = op
        return op

    def barrier(self):
        bar = {}
        for e, op in self.last.items():
            if op.dma_sem is None:
                bar[("e", e)] = op
                op.need_inc = True
        for k, ent in self.dma_sems.items():
            bar[("d", k)] = ent[1]
        self.bar = bar

    def emit(self, final_waits=()):
        nc = self.nc
        sems = {e: nc.alloc_semaphore(name="sem_" + e) for e in self.ENGS}
        for k, ent in self.dma_sems.items():
            ent[0] = nc.alloc_semaphore(name="dsem_%s" % (k,))
        for e in self.ENGS:
            c = 0
            for op in self.per_eng[e]:
                if op.need_inc:
                    c += 1
                    op.inc_cnt = c
        per_eng = self.per_eng
        dma_sems = self.dma_sems

        def run(e, engobj):
            seen = {}
            for op in per_eng[e]:
                for k, d in op.deps.items():
                    if k[0] == "d":
                        sem, val = dma_sems[k[1]][0], d
                    else:
                        sem, val = sems[k[1]], d.inc_cnt
                    if seen.get(k, 0) >= val:
                        continue
                    seen[k] = val
                    engobj.wait_ge(sem, val)
                ins = op.fn(engobj)
                if op.dma_sem is not None:
                    ins.then_inc(dma_sems[op.dma_sem][0], 16)
                elif op.need_inc:
                    ins.then_inc(sems[e], 1)
            if e == "sp":
                for k in final_waits:
                    engobj.wait_ge(dma_sems[k][0], dma_sems[k][1])

        with nc.Block() as block:
            @block.tensor
            def _(eng):
                run("pe", eng)

            @block.scalar
            def _(eng):
                run("act", eng)

            @block.vector
            def _(eng):
                run("dve", eng)

            @block.gpsimd
            def _(eng):
                run("pool", eng)

            @block.sync
            def _(eng):
                run("sp", eng)


class Arena:
    def __init__(self, nc, words=51200):
        self.nc = nc
        self.big = nc.alloc_sbuf_tensor("arena", [128, words], F32)
        self.off = 0
        self.limit = words

    def mark(self):
        return self.off

    def reset(self, m):
        self.off = m

    def alloc(self, shape, dtype, name=None):
        n = int(np.prod(shape[1:]))
        words = n if dtype == F32 else (n + 1) // 2
        words = (words + 15) // 16 * 16
        assert self.off + words <= self.limit, ("SBUF overflow", name, self.off, words)
        v = self.big[:, self.off:self.off + words]
        self.off += words
        if dtype != F32:
            v = v.bitcast(dtype)
        v = v[:, 0:n]
        if len(shape) == 3:
            v = v.rearrange("p (a b) -> p a b", b=shape[2])
        elif len(shape) == 4:
            v = v.rearrange("p (a b c) -> p a b c", b=shape[2], c=shape[3])
        return v


class Psum:
    def __init__(self, nc):
        self.t = nc.alloc_psum_tensor("psum_all", [128, 4096], F32)

    def bank(self, b, n=1):
        return self.t[:, 512 * b:512 * (b + n)]

EPS = 1e-6
SCALE = 128 ** -0.5

C_C, C_BADA, C_NW, C_CW, C_CB, C_BG, C_FNW, C_GM, NCOL = 0, 8, 32, 40, 80, 90, 106, 114, 116
R_DTB, R_ALOG, R_BF, R_D, R_SW, NROW = 0, 16, 32, 40, 1064, 2088


DBG = [0]


def dbg(n):
    if DBG[0] == n:
        MUTE[0] = True


def build(S_TOK, phases="12BC"):
    NT = S_TOK // 512
    NB = S_TOK // 128
    nc = bass.Bass("TRN2", target_bir_lowering=False)

    def din(name, shape, dt=F32):
        return nc.dram_tensor(name, shape, dt, kind="ExternalInput").ap()

    xT = din("xT", [1024, S_TOK])
    wA1 = din("wA1", [1024, 3080])
    wA2 = din("wA2", [1024, 2320])
    wC = din("wC", [1024, 3072])
    wada = din("wada", [1024, 3072])
    wps = din("wps", [1024, 1024])
    wpa = din("wpa", [1024, 1024])
    wout = din("wout", [1024, 1024])
    cols_d = din("cols", [128, NCOL])
    rows_d = din("rows", [128, NROW])
    masks_d = din("masks", [128, 640])
    outT = nc.dram_tensor("outT", [1024, S_TOK], F32, kind="ExternalOutput").ap()
    KT = nc.dram_tensor("KT", [8, 128, S_TOK], BF16, kind="Internal").ap()
    QT = nc.dram_tensor("QT", [8, 128, S_TOK], BF16, kind="Internal").ap()
    VS = nc.dram_tensor("VS", [S_TOK, 1024], BF16, kind="Internal").ap()
    YA = nc.dram_tensor("YA", [1024, S_TOK], BF16, kind="Internal").ap()
    YW = nc.dram_tensor("YW", [1024, S_TOK], BF16, kind="Internal").ap()

    S = Sched(nc)
    A = Arena(nc)
    PS = Psum(nc)
    pbank = [(PS.bank(b), Res("pb%d" % b)) for b in range(8)]
    pb_rr = [0]

    def nextbank():
        b = pbank[pb_rr[0] % 8]
        pb_rr[0] += 1
        return b

    def MM(out, lhsT, rhs, st, sp, R, W):
        S.add("pe", lambda e: e.matmul(out, lhsT=lhsT, rhs=rhs, start=st, stop=sp), R, W)

    def TR(out, in_, ident, R, W):
        S.add("pe", lambda e: e.transpose(out, in_, ident), R, W)

    def ACT(out, in_, func, R, W, bias=None, scale=None, accum=None):
        kw = {}
        if bias is not None:
            kw["bias"] = bias
        if scale is not None:
            kw["scale"] = scale
        if accum is not None:
            kw["accum_out"] = accum
        S.add("act", lambda e: e.activation(out=out, in_=in_, func=func, **kw), R, W)

    def TT(eng, out, in0, in1, op, R, W):
        S.add(eng, lambda e: e.tensor_tensor(out=out, in0=in0, in1=in1, op=op), R, W)

    def TS(eng, out, in0, s1, s2, op0, op1, R, W):
        if op1 is None:
            S.add(eng, lambda e: e.tensor_scalar(out=out, in0=in0, scalar1=s1, scalar2=None, op0=op0), R, W)
        else:
            S.add(eng, lambda e: e.tensor_scalar(out=out, in0=in0, scalar1=s1, scalar2=s2, op0=op0, op1=op1), R, W)

    def STT(eng, out, in0, scalar, in1, op0, op1, R, W):
        S.add(eng, lambda e: e.scalar_tensor_tensor(out=out, in0=in0, scalar=scalar, in1=in1, op0=op0, op1=op1), R, W)

    def CP(eng, out, in_, R, W):
        if eng == "act":
            ACT(out, in_, AF.Copy, R, W)
        else:
            S.add(eng, lambda e: e.tensor_copy(out=out, in_=in_), R, W)

    def MEMSET(eng, ap, val, W):
        S.add(eng, lambda e: e.memset(ap, val), [], W)

    def DMA(q, out, in_, R, W, key):
        S.add(q, lambda e: e.dma_start(out=out, in_=in_), R, W, dma=key)

    cols = A.alloc([128, NCOL], F32, "cols")
    rows = A.alloc([128, NROW], F32, "rows")
    mk = A.alloc([128, 640], F32, "masks")
    r_const = Res("const")
    DMA("sp", cols, cols_d, [], [r_const], "c0")
    DMA("sp", rows, rows_d, [], [r_const], "c1")
    DMA("sp", mk, masks_d, [], [r_const], "c2")
    U_f, SL_f, half_f, ones_f, id_f = (mk[:, 128 * i:128 * (i + 1)] for i in range(5))
    mkb = A.alloc([128, 384], BF16, "masksb")
    r_cb = Res("constb")
    CP("dve", mkb[:, 0:128], U_f, [r_const], [r_cb])
    CP("dve", mkb[:, 128:256], ones_f, [r_const], [r_cb])
    CP("dve", mkb[:, 256:384], id_f, [r_const], [r_cb])
    U_b, ones_b, id_b = mkb[:, 0:128], mkb[:, 128:256], mkb[:, 256:384]
    adac = A.alloc([128, 32], F32, "adac")
    r_ada = Res("ada")
    aneg = A.alloc([128, 16], F32, "aneg")
    Ftok = A.alloc([128, NB, 8], F32, "Ftok")
    Cend = A.alloc([128, NB, 8], F32, "Cend")
    Cmid = A.alloc([128, NB, 8], F32, "Cmid")
    nCend = A.alloc([128, NB, 8], F32, "nCend")
    zero8 = A.alloc([128, 8], F32, "zero8")
    r_F = Res("F")
    MEMSET("pool", zero8, 0.0, [r_F])
    stage = [(A.alloc([128, 1024], F32, "stage%d" % i), Res("stage%d" % i)) for i in range(2)]
    st_rr = [0]
    base_mark = A.mark()

    def load_w(dst, r_dst, src, ncols):
        for k in range(8):
            c0 = 0
            while c0 < ncols:
                n = min(1024, ncols - c0)
                stg, r_s = stage[st_rr[0] % 2]
                key = "stage%d" % (st_rr[0] % 2)
                st_rr[0] += 1
                DMA("sp", stg[:, 0:n], src[k * 128:(k + 1) * 128, c0:c0 + n], [], [r_s], key)
                CP("pool" if (st_rr[0] % 2) else "dve", dst[:, k, c0:c0 + n], stg[:, 0:n], [r_s], [r_dst])
                c0 += n

    m0 = A.mark()
    wada_sb = A.alloc([128, 8, 3072], F32, "wada")
    r_wada = Res("wada")
    for k in range(8):
        DMA("sp", wada_sb[:, k, :], wada[k * 128:(k + 1) * 128, :], [], [r_wada], "wada")
    sc = A.alloc([128, 8], F32, "silu_c")
    r_sc = Res("sc")
    ACT(sc, cols[:, C_C:C_C + 8], AF.Silu, [r_const], [r_sc])
    pb, r_pb = nextbank()
    for eb in range(24):
        for k in range(8):
            MM(pb[:, eb:eb + 1], wada_sb[:, k, eb * 128:(eb + 1) * 128], sc[:, k:k + 1], k == 0, k == 7,
               [r_wada, r_sc], [r_pb])
    adaf = A.alloc([128, 24], F32, "adaf")
    TT("dve", adaf, pb[:, 0:24], cols[:, C_BADA:C_BADA + 24], ALU.add, [r_pb, r_const], [r_ada])
    TS("dve", adac[:, 24:32], adaf[:, 8:16], 1.0, None, ALU.add, None, [r_ada], [r_ada])
    TT("dve", adac[:, 0:8], adac[:, 24:32], cols[:, C_NW:C_NW + 8], ALU.mult, [r_ada, r_const], [r_ada])
    CP("dve", adac[:, 8:16], adaf[:, 0:8], [r_ada], [r_ada])
    CP("dve", adac[:, 16:24], adaf[:, 16:24], [r_ada], [r_ada])
    ACT(aneg, rows[:, R_ALOG:R_ALOG + 16], AF.Exp, [r_const], [r_ada])
    TS("dve", aneg, aneg, -1.0, None, ALU.mult, None, [r_ada], [r_ada])
    S.barrier()
    A.reset(m0)

    def norm_bufs():
        d = {}
        d["xin"] = A.alloc([128, 8, 512], F32, "xin"); d["r_xin"] = Res("xin")
        d["sq"] = A.alloc([128, 8, 512], BF16, "sq"); d["r_sq"] = Res("sq")
        d["hT"] = A.alloc([128, 8, 512], BF16, "hT"); d["r_hT"] = Res("hT")
        d["tmp"] = [(A.alloc([128, 512], F32, "tmp"), Res("tmp")) for _ in range(2)]
        d["rstd"] = A.alloc([128, 512], F32, "rstd"); d["r_rstd"] = Res("rstd")
        return d

    def rms_rstd(nb, src, r_src):
        for k in range(8):
            ACT(nb["sq"][:, k, :], src[:, k, :], AF.Square, [r_src], [nb["r_sq"]])
        pb, r_pb = nextbank()
        for k in range(8):
            MM(pb, ones_b, nb["sq"][:, k, :], k == 0, k == 7, [nb["r_sq"], r_cb], [r_pb])
        TS("dve", nb["rstd"], pb, 1.0 / 1024, EPS, ALU.mult, ALU.add, [r_pb], [nb["r_rstd"]])
        ACT(nb["rstd"], nb["rstd"], AF.Ln, [nb["r_rstd"]], [nb["r_rstd"]])
        ACT(nb["rstd"], nb["rstd"], AF.Exp, [nb["r_rstd"]], [nb["r_rstd"]], scale=-0.5)

    def load_x(nb, i):
        DMA("sp", nb["xin"], xT.rearrange("(k p) t -> p k t", p=128)[:, :, i * 512:(i + 1) * 512],
            [], [nb["r_xin"]], "xin")

    def norm_tile(nb):
        rms_rstd(nb, nb["xin"], nb["r_xin"])
        for k in range(8):
            tmp, r_tmp = nb["tmp"][k % 2]
            TT("dve", tmp, nb["xin"][:, k, :], nb["rstd"], ALU.mult, [nb["r_xin"], nb["r_rstd"]], [r_tmp])
            ACT(nb["hT"][:, k, :], tmp, AF.Identity, [r_tmp, r_ada], [nb["r_hT"]],
                bias=adac[:, 8 + k:9 + k], scale=adac[:, k:k + 1])

    ev_rr = [0]

    def evac(out, in_, R, W):
        e = "act" if ev_rr[0] % 2 == 0 else "dve"
        ev_rr[0] += 1
        CP(e, out, in_, R, W)

    if "1" in phases:
        m1 = A.mark()
        w1 = A.alloc([128, 8, 3080], BF16, "w1"); r_w1 = Res("w1")
        load_w(w1, r_w1, wA1, 3080)
        nb = norm_bufs()
        kq = A.alloc([128, 16, 512], BF16, "kq"); r_kq = Res("kq")
        vt = [(A.alloc([128, 1024], BF16, "vt"), Res("vt")) for _ in range(2)]
        fsm = A.alloc([128, 32], F32, "fsm"); r_fsm = Res("fsm")
        for i in range(NT):
            load_x(nb, i)
            norm_tile(nb)
            hT, r_hT = nb["hT"], nb["r_hT"]
            for blk in range(16):
                pb, r_pb = nextbank()
                for k in range(8):
                    MM(pb, w1[:, k, blk * 128:(blk + 1) * 128], hT[:, k, :], k == 0, k == 7, [r_w1, r_hT], [r_pb])
                evac(kq[:, blk, :], pb, [r_pb], [r_kq])
            DMA("pool", QT.rearrange("h d t -> d h t")[:, :, i * 512:(i + 1) * 512], kq[:, 0:8, :], [r_kq], [], "kq")
            DMA("pool", KT.rearrange("h d t -> d h t")[:, :, i * 512:(i + 1) * 512], kq[:, 8:16, :], [r_kq], [], "kq")
            for c in range(4):
                blk = i * 4 + c
                tsl = slice(c * 128, (c + 1) * 128)
                v_sb, r_v = vt[c % 2]
                for hf in range(2):
                    pb, r_pb = nextbank()
                    for k in range(8):
                        MM(pb, hT[:, k, tsl], w1[:, k, 2048 + hf * 512:2048 + (hf + 1) * 512], k == 0, k == 7,
                           [r_w1, r_hT], [r_pb])
                    evac(v_sb[:, hf * 512:(hf + 1) * 512], pb, [r_pb], [r_v])
                DMA("pool", VS[blk * 128:(blk + 1) * 128, :], v_sb, [r_v], [], "vt%d" % (c % 2))
                pb, r_pb = nextbank()
                for k in range(8):
                    MM(pb[:, 0:8], hT[:, k, tsl], w1[:, k, 3072:3080], k == 0, k == 7, [r_w1, r_hT], [r_pb])
                TT("dve", fsm[:, 0:8], pb[:, 0:8], rows[:, R_BF:R_BF + 8], ALU.add, [r_pb, r_const], [r_fsm])
                ACT(fsm[:, 8:16], fsm[:, 0:8], AF.Exp, [r_fsm], [r_fsm], scale=-1.0)
                ACT(fsm[:, 16:24], fsm[:, 8:16], AF.Ln, [r_fsm], [r_fsm], bias=1.0)
                TS("dve", fsm[:, 24:32], fsm[:, 16:24], -1.0, None, ALU.mult, None, [r_fsm], [r_fsm])
                lf = fsm[:, 24:32]
                pb, r_pb = nextbank()
                MM(pb[:, 0:8], U_f, lf, True, True, [r_fsm, r_const], [r_pb])
                MM(pb[:, 8:16], ones_f, lf, True, True, [r_fsm, r_const], [r_pb])
                MM(pb[:, 16:24], half_f, lf, True, True, [r_fsm, r_const], [r_pb])
                carry = zero8 if blk == 0 else Cend[:, blk - 1, :]
                TT("dve", Ftok[:, blk, :], pb[:, 0:8], carry, ALU.add, [r_pb, r_F], [r_F])
                TT("dve", Cmid[:, blk, :], pb[:, 16:24], carry, ALU.add, [r_pb, r_F], [r_F])
                TT("dve", Cend[:, blk, :], pb[:, 8:16], carry, ALU.add, [r_pb, r_F], [r_F])
                TS("dve", nCend[:, blk, :], Cend[:, blk, :], -1.0, None, ALU.mult, None, [r_F], [r_F])
        S.barrier()
        A.reset(m1)

    if "2" in phases:
        m1 = A.mark()
        w2 = A.alloc([128, 8, 2320], BF16, "w2"); r_w2 = Res("w2")
        load_w(w2, r_w2, wA2, 2320)
        nb = norm_bufs()
        hT, r_hT = nb["hT"], nb["r_hT"]
        ubuf = A.alloc([128, 10, 515], F32, "ubuf"); r_u = Res("ubuf")
        MEMSET("pool", ubuf[:, :, 0:3], 0.0, [r_u])
        cacc = [(A.alloc([128, 512], F32, "cacc"), Res("cacc")) for _ in range(2)]
        xsT = A.alloc([128, 10, 512], BF16, "xsT"); r_xsT = Res("xsT")
        bcg = A.alloc([128, 4, 512], BF16, "bcg"); r_bcg = Res("bcg")
        xs_tok = A.alloc([128, 1024], BF16, "xs_tok"); r_xst = Res("xs_tok")
        B_tok = A.alloc([128, 128], BF16, "B_tok"); r_Bt = Res("B_tok")
        sm = A.alloc([128, 160], F32, "sm"); r_sm = Res("sm")
        zs = A.alloc([128, 1024], BF16, "zs"); r_zs = Res("zs")
        Rb = A.alloc([128, 16, 128], F32, "Rb"); r_R = Res("R")
        cbm = A.alloc([128, 2, 128], F32, "cbm"); r_cbm = Res("cbm")
        eseg = [(A.alloc([128, 512], F32, "eseg"), Res("eseg")) for _ in range(2)]
        MT = A.alloc([128, 16, 128], BF16, "MT"); r_MT = Res("MT")
        xdt = A.alloc([128, 1024], BF16, "xdt"); r_xdt = Res("xdt")
        xdtd = A.alloc([128, 1024], BF16, "xdtd"); r_xdtd = Res("xdtd")
        Sst = A.alloc([128, 512], F32, "Sst"); r_S = Res("Sst")
        Sbf = A.alloc([128, 512], BF16, "Sbf"); r_Sb = Res("Sbf")
        y1 = A.alloc([128, 1024], F32, "y1"); r_y1 = Res("y1")
        y2 = A.alloc([128, 1024], F32, "y2"); r_y2 = Res("y2")
        junk = A.alloc([128, 1024], BF16, "junk"); r_junk = Res("junk")
        Yn = A.alloc([128, 1024], BF16, "Yn"); r_Yn = Res("Yn")
        YwT = A.alloc([128, 8, 512], BF16, "YwT"); r_YwT = Res("YwT")
        MEMSET("pool", Sst, 0.0, [r_S])
        MEMSET("pool", Sbf, 0.0, [r_Sb])
        dtraw, dtx, dt, a_t, acs, atot, eacs, dd, dec, eatot, ssq = (
            sm[:, 0:16], sm[:, 16:32], sm[:, 32:48], sm[:, 48:64], sm[:, 64:80], sm[:, 80:96],
            sm[:, 96:112], sm[:, 112:128], sm[:, 128:144], sm[:, 144:160], None)
        sm2 = A.alloc([128, 16], F32, "sm2"); r_sm2 = Res("sm2")
        for i in range(NT if DBG[0] == 0 else 1):
            load_x(nb, i)
            norm_tile(nb)
            dbg(1)
            if i > 0:
                CP("pool", ubuf[:, :, 0:3], ubuf[:, :, 512:515], [r_u], [r_u])
            for blk in range(10):
                pb, r_pb = nextbank()
                for k in range(8):
                    MM(pb, w2[:, k, blk * 128:(blk + 1) * 128], hT[:, k, :], k == 0, k == 7, [r_w2, r_hT], [r_pb])
                evac(ubuf[:, blk, 3:515], pb, [r_pb], [r_u])
            dbg(2)
            for blk in range(10):
                ca, r_ca = cacc[blk % 2]
                cw = C_CW + blk * 4
                TS("dve", ca, ubuf[:, blk, 0:512], cols[:, cw:cw + 1], cols[:, C_CB + blk:C_CB + blk + 1],
                   ALU.mult, ALU.add, [r_u, r_const], [r_ca])
                for tap in range(1, 4):
                    STT("dve", ca, ubuf[:, blk, tap:tap + 512], cols[:, cw + tap:cw + tap + 1], ca,
                        ALU.mult, ALU.add, [r_u, r_const, r_ca], [r_ca])
                ACT(xsT[:, blk, :], ca, AF.Silu, [r_ca], [r_xsT])
            for g in range(2):
                TS("pool", bcg[:, g, :], xsT[:, 8, :], cols[:, C_GM + g:C_GM + g + 1], None, ALU.mult, None,
                   [r_xsT, r_const], [r_bcg])
                TS("pool", bcg[:, 2 + g, :], xsT[:, 9, :], cols[:, C_GM + g:C_GM + g + 1], None, ALU.mult, None,
                   [r_xsT, r_const], [r_bcg])
            dbg(3)
            for c in range(4):
                tsl = slice(c * 128, (c + 1) * 128)
                for hf in range(2):
                    pb, r_pb = nextbank()
                    for k in range(8):
                        MM(pb, hT[:, k, tsl], w2[:, k, 1280 + hf * 512:1280 + (hf + 1) * 512], k == 0, k == 7,
                           [r_w2, r_hT], [r_pb])
                    ACT(zs[:, hf * 512:(hf + 1) * 512], pb, AF.Silu, [r_pb], [r_zs])
                pb, r_pb = nextbank()
                for k in range(8):
                    MM(pb[:, 0:16], hT[:, k, tsl], w2[:, k, 2304:2320], k == 0, k == 7, [r_w2, r_hT], [r_pb])
                TT("dve", dtraw, pb[:, 0:16], rows[:, R_DTB:R_DTB + 16], ALU.add, [r_pb, r_const], [r_sm])
                ACT(dtx, dtraw, AF.Exp, [r_sm], [r_sm])
                ACT(dt, dtx, AF.Ln, [r_sm], [r_sm], bias=1.0)
                TT("dve", a_t, dt, aneg, ALU.mult, [r_sm, r_ada], [r_sm])
                dbg(4)
                pbt, r_pbt = nextbank()
                pbt_b = pbt.bitcast(BF16)
                for blk in range(8):
                    TR(pbt_b[:, blk * 128:(blk + 1) * 128], xsT[:, blk, tsl], id_b, [r_xsT, r_cb], [r_pbt])
                evac(xs_tok, pbt_b, [r_pbt], [r_xst])
                pbt, r_pbt = nextbank()
                pbt_b = pbt.bitcast(BF16)
                TR(pbt_b[:, 0:128], xsT[:, 8, tsl], id_b, [r_xsT, r_cb], [r_pbt])
                evac(B_tok, pbt_b[:, 0:128], [r_pbt], [r_Bt])
                dbg(5)
                pb, r_pb = nextbank()
                MM(pb[:, 0:16], U_f, a_t, True, True, [r_sm, r_const], [r_pb])
                MM(pb[:, 16:32], ones_f, a_t, True, True, [r_sm, r_const], [r_pb])
                CP("dve", sm[:, 64:96], pb[:, 0:32], [r_pb], [r_sm])
                ACT(eacs, acs, AF.Exp, [r_sm], [r_sm])
                TT("dve", dd, atot, acs, ALU.subtract, [r_sm], [r_sm])
                ACT(dec, dd, AF.Exp, [r_sm], [r_sm])
                ACT(eatot, atot, AF.Exp, [r_sm], [r_sm])
                dbg(6)
                TT("dve", Rb, U_f.unsqueeze(1).to_broadcast([128, 16, 128]),
                   a_t.unsqueeze(2).to_broadcast([128, 16, 128]), ALU.mult, [r_sm, r_const], [r_R])
                dbg(61)
                pb, r_pb = nextbank()
                for g in range(2):
                    MM(pb[:, g * 128:(g + 1) * 128], bcg[:, g, tsl], xsT[:, 9, tsl],
                       True, True, [r_xsT, r_bcg], [r_pb])
                dbg(62)
                TT("dve", cbm, pb[:, 0:256].rearrange("p (g l) -> p g l", g=2),
                   U_f.unsqueeze(1).to_broadcast([128, 2, 128]), ALU.mult, [r_pb, r_const], [r_cbm])
                dbg(7)
                for q4 in range(4):
                    pb, r_pb = nextbank()
                    MM(pb, SL_f, Rb[:, 4 * q4:4 * q4 + 4, :].rearrange("p h l -> p (h l)"), True, True,
                       [r_R, r_const], [r_pb])
                    es, r_es = eseg[q4 % 2]
                    ACT(es, pb, AF.Exp, [r_pb], [r_es])
                    g = q4 // 2
                    TT("dve", MT[:, 4 * q4:4 * q4 + 4, :], es.rearrange("p (h l) -> p h l", h=4),
                       cbm[:, g:g + 1, :].to_broadcast([128, 4, 128]), ALU.mult, [r_es, r_cbm], [r_MT])
                dbg(8)
                TT("dve", xdt.rearrange("p (h d) -> p h d", h=16), xs_tok.rearrange("p (h d) -> p h d", h=16),
                   dt.unsqueeze(2).to_broadcast([128, 16, 64]), ALU.mult, [r_xst, r_sm], [r_xdt])
                TT("pool", xdtd.rearrange("p (h d) -> p h d", h=16), xdt.rearrange("p (h d) -> p h d", h=16),
                   dec.unsqueeze(2).to_broadcast([128, 16, 64]), ALU.mult, [r_xdt, r_sm], [r_xdtd])
                dbg(9)
                yd = []
                for hf in range(2):
                    pb, r_pb = nextbank()
                    for hh in range(8):
                        h = hf * 8 + hh
                        MM(pb[:, hh * 64:(hh + 1) * 64], MT[:, h, :], xdt[:, h * 64:(h + 1) * 64], True, True,
                           [r_MT, r_xdt], [r_pb])
                    yd.append((pb, r_pb))
                yo = []
                for g in range(2):
                    pb, r_pb = nextbank()
                    MM(pb, bcg[:, 2 + g, tsl], Sbf, True, True, [r_bcg, r_Sb], [r_pb])
                    yo.append((pb, r_pb))
                dbg(10)
                for g in range(2):
                    sl = slice(g * 512, (g + 1) * 512)
                    TT("dve", y1[:, sl].rearrange("p (h d) -> p h d", h=8),
                       yo[g][0].rearrange("p (h d) -> p h d", h=8),
                       eacs[:, 8 * g:8 * g + 8].unsqueeze(2).to_broadcast([128, 8, 64]), ALU.mult,
                       [yo[g][1], r_sm], [r_y1])
                    TT("dve", y1[:, sl], yd[g][0], y1[:, sl], ALU.add, [yd[g][1], r_y1], [r_y1])
                TT("pool", y2, xs_tok, rows[:, R_D:R_D + 1024], ALU.mult, [r_xst, r_const], [r_y2])
                TT("pool", y1, y1, y2, ALU.add, [r_y1, r_y2], [r_y1])
                TT("dve", y2, y1, zs, ALU.mult, [r_y1, r_zs], [r_y2])
                ACT(junk, y2, AF.Square, [r_y2], [r_junk, r_sm2], accum=sm2[:, 0:1])
                TS("dve", sm2[:, 1:2], sm2[:, 0:1], 1.0 / 1024, EPS, ALU.mult, ALU.add, [r_sm2], [r_sm2])
                ACT(sm2[:, 2:3], sm2[:, 1:2], AF.Ln, [r_sm2], [r_sm2])
                ACT(sm2[:, 3:4], sm2[:, 2:3], AF.Exp, [r_sm2], [r_sm2], scale=-0.5)
                STT("dve", Yn, y2, sm2[:, 3:4], rows[:, R_SW:R_SW + 1024], ALU.mult, ALU.mult,
                    [r_y2, r_sm2, r_const], [r_Yn])
                dbg(11)
                pbt, r_pbt = nextbank()
                pbt_b = pbt.bitcast(BF16)
                for blk in range(8):
                    TR(pbt_b[:, blk * 128:(blk + 1) * 128], Yn[:, blk * 128:(blk + 1) * 128], id_b, [r_Yn, r_cb], [r_pbt])
                evac(YwT[:, :, tsl], pbt_b.rearrange("p (b t) -> p b t", b=8), [r_pbt], [r_YwT])
                dbg(12)
                for g in range(2):
                    pb, r_pb = nextbank()
                    MM(pb, B_tok, xdtd[:, g * 512:(g + 1) * 512], True, True, [r_Bt, r_xdtd], [r_pb])
                    ps_ = slice(64 * g, 64 * g + 64)
                    TT("dve", Sst[ps_, :].rearrange("p (h d) -> p h d", h=8),
                       Sst[ps_, :].rearrange("p (h d) -> p h d", h=8),
                       eatot[ps_, 8 * g:8 * g + 8].unsqueeze(2).to_broadcast([64, 8, 64]), ALU.mult,
                       [r_S, r_sm], [r_S])
                    TT("dve", Sst[ps_, :], pb[ps_, :], Sst[ps_, :], ALU.add, [r_pb, r_S], [r_S])
                    CP("pool", Sbf[ps_, :], Sst[ps_, :], [r_S], [r_Sb])
            DMA("pool", YW.rearrange("(k p) t -> p k t", p=128)[:, :, i * 512:(i + 1) * 512], YwT, [r_YwT], [], "YwT")
        S.barrier()
        A.reset(m1)

    if "B" in phases:
        m1 = A.mark()
        hb = []
        for s_ in range(2):
            hb.append(dict(K=A.alloc([128, S_TOK], BF16, "Kh"), V=A.alloc([128, NB, 128], BF16, "Vh"),
                           Q=A.alloc([128, S_TOK], BF16, "Qh"), r=Res("hb%d" % s_)))
        bfar = A.alloc([128, NB], F32, "bfar"); r_bfar = Res("bfar")
        bnear = A.alloc([128, 4, 4], F32, "bnear"); r_bnear = Res("bnear")
        fac = A.alloc([128, 4], F32, "fac"); r_fac = Res("fac")
        pTf = [(A.alloc([128, 512], BF16, "pTf"), Res("pTf")) for _ in range(3)]
        pTn = [(A.alloc([128, 512], BF16, "pTn"), Res("pTn")) for _ in range(4)]
        for kb in range(1, 4):
            MEMSET("pool", pTn[kb][0][:, 0:128 * kb], 0.0, [pTn[kb][1]])
        on_sb = A.alloc([128, 512], F32, "on_sb"); r_on = Res("on")
        dn_sb = A.alloc([128, 512], F32, "dn_sb"); r_dn = Res("dn")
        yat = [(A.alloc([128, 512], BF16, "yat"), Res("yat")) for _ in range(2)]
        st_b = [pbank[0], pbank[1]]
        ofar, r_ofar = pbank[2]
        dfar, r_dfar = pbank[3]
        onear, r_onear = pbank[4]
        dnear, r_dnear = pbank[5]

        def load_head(h):
            d = hb[h % 2]
            key = "hb%d" % (h % 2)
            DMA("sp", d["K"], KT[h], [], [d["r"]], key)
            DMA("sp", d["Q"], QT[h], [], [d["r"]], key)
            vsrc = VS.rearrange("(b p) (h d) -> p b h d", p=128, h=8)
            nstep = max(1, NB // 8)
            for b0 in range(0, NB, nstep):
                DMA("sp", d["V"][:, b0:b0 + nstep, :], vsrc[:, b0:b0 + nstep, h, :], [], [d["r"]], key)

        load_head(0)
        pf = 0
        sti = 0
        for h in range(8):
            if h + 1 < 8:
                load_head(h + 1)
            d = hb[h % 2]
            Kh, Vh, Qh, r_h = d["K"], d["V"], d["Q"], d["r"]
            for i in range(NT):
                T0 = i * 512
                qs = slice(T0, T0 + 512)
                nfar = 4 * i
                if i > 0:
                    TS("dve", bfar[:, 0:nfar], Ftok[:, 0:nfar, h], -1.0, Cend[:, nfar - 1, h:h + 1],
                       ALU.mult, ALU.add, [r_F], [r_bfar])
                    ACT(fac, Cmid[:, nfar:nfar + 4, h], AF.Exp, [r_F], [r_fac], bias=nCend[:, nfar - 1, h:h + 1])
                TT("dve", bnear, Cmid[:, nfar:nfar + 4, h].unsqueeze(1).to_broadcast([128, 4, 4]),
                   Ftok[:, nfar:nfar + 4, h].unsqueeze(2).to_broadcast([128, 4, 4]), ALU.subtract,
                   [r_F], [r_bnear])
                for c in range(nfar):
                    stp, r_st = st_b[sti % 2]
                    sti += 1
                    MM(stp, Kh[:, c * 128:(c + 1) * 128], Qh[:, qs], True, True, [r_h], [r_st])
                    pT, r_pT = pTf[pf % 3]
                    pf += 1
                    ACT(pT, stp, AF.Exp, [r_st, r_bfar], [r_pT], bias=bfar[:, c:c + 1], scale=SCALE)
                    MM(ofar, Vh[:, c, :], pT, c == 0, c == nfar - 1, [r_h, r_pT], [r_ofar])
                    MM(dfar, ones_b, pT, c == 0, c == nfar - 1, [r_cb, r_pT], [r_dfar])
                for kb in range(4):
                    c = nfar + kb
                    stp, r_st = st_b[sti % 2]
                    sti += 1
                    MM(stp, Kh[:, c * 128:(c + 1) * 128], Qh[:, qs], True, True, [r_h], [r_st])
                    pT, r_pT = pTn[kb]
                    for j in range(kb, 4):
                        js = slice(128 * j, 128 * (j + 1))
                        ACT(pT[:, js], stp[:, js], AF.Exp, [r_st, r_bnear], [r_pT], bias=bnear[:, kb, j:j + 1],
                            scale=SCALE)
                    ks = slice(128 * kb, 128 * (kb + 1))
                    TT("pool", pT[:, ks], pT[:, ks], U_b, ALU.mult, [r_pT, r_cb], [r_pT])
                    MM(onear, Vh[:, c, :], pT, kb == 0, kb == 3, [r_h, r_pT], [r_onear])
                    MM(dnear, ones_b, pT, kb == 0, kb == 3, [r_cb, r_pT], [r_dnear])
                CP("act", on_sb, onear, [r_onear], [r_on])
                CP("act", dn_sb, dnear, [r_dnear], [r_dn])
                if i > 0:
                    for j in range(4):
                        js = slice(128 * j, 128 * (j + 1))
                        STT("dve", on_sb[:, js], ofar[:, js], fac[:, j:j + 1], on_sb[:, js], ALU.mult, ALU.add,
                            [r_ofar, r_fac, r_on], [r_on])
                        STT("dve", dn_sb[:, js], dfar[:, js], fac[:, j:j + 1], dn_sb[:, js], ALU.mult, ALU.add,
                            [r_dfar, r_fac, r_dn], [r_dn])
                S.add("dve", lambda e, o=dn_sb: e.reciprocal(out=o, in_=o), [r_dn], [r_dn])
                ya, r_ya = yat[(h * NT + i) % 2]
                TT("dve", ya, on_sb, dn_sb, ALU.mult, [r_on, r_dn], [r_ya])
                DMA("pool", YA[h * 128:(h + 1) * 128, qs], ya, [r_ya], [], "yat%d" % ((h * NT + i) % 2))
        S.barrier()
        A.reset(m1)

    MUTE[0] = False
    final = []
    if "C" in phases:
        w3 = A.alloc([128, 8, 3072], BF16, "w3"); r_w3 = Res("w3")
        load_w(w3, r_w3, wC, 3072)
        wp = A.alloc([128, 3, 8, 1024], BF16, "wp"); r_wp = Res("wp")
        for j, src in enumerate((wps, wpa, wout)):
            load_w(wp[:, j], r_wp, src, 1024)
        nb = norm_bufs()
        hT, r_hT = nb["hT"], nb["r_hT"]
        xin, r_xin = nb["xin"], nb["r_xin"]
        YAt = A.alloc([128, 8, 512], BF16, "YAt"); r_YAt = Res("YAt")
        YWt = A.alloc([128, 8, 512], BF16, "YWt"); r_YWt = Res("YWt")
        yab = A.alloc([128, 8, 512], BF16, "yab"); r_yab = Res("yab")
        mg = A.alloc([128, 8, 512], BF16, "mg"); r_mg = Res("mg")
        t4 = [(A.alloc([128, 512], F32, "t4"), Res("t4")) for _ in range(2)]
        for i in range(NT):
            ts_ = slice(i * 512, (i + 1) * 512)
            load_x(nb, i)
            DMA("sp", YAt, YA.rearrange("(k p) t -> p k t", p=128)[:, :, ts_], [], [r_YAt], "YAt")
            DMA("sp", YWt, YW.rearrange("(k p) t -> p k t", p=128)[:, :, ts_], [], [r_YWt], "YWt")
            norm_tile(nb)
            for cb_ in range(8):
                pb, r_pb = nextbank()
                for k in range(8):
                    MM(pb, w3[:, k, 2048 + cb_ * 128:2048 + (cb_ + 1) * 128], hT[:, k, :], k == 0, k == 7,
                       [r_w3, r_hT], [r_pb])
                t, r_t = t4[cb_ % 2]
                ACT(t, pb, AF.Silu, [r_pb], [r_t])
                TT("dve", yab[:, cb_, :], YAt[:, cb_, :], t, ALU.mult, [r_YAt, r_t], [r_yab])
            for eb in range(8):
                gs, r_gs = t4[0]
                ga, r_ga = t4[1]
                for (gt, r_gt, off) in ((gs, r_gs, 0), (ga, r_ga, 1024)):
                    pb, r_pb = nextbank()
                    for k in range(8):
                        MM(pb, w3[:, k, off + eb * 128:off + (eb + 1) * 128], hT[:, k, :], k == 0, k == 7,
                           [r_w3, r_hT], [r_pb])
                    bcol = C_BG + (off // 128) + eb
                    ACT(gt, pb, AF.Sigmoid, [r_pb, r_const], [r_gt], bias=cols[:, bcol:bcol + 1])
                pbs, r_pbs = nextbank()
                for k in range(8):
                    MM(pbs, wp[:, 0, k, eb * 128:(eb + 1) * 128], YWt[:, k, :], k == 0, k == 7, [r_wp, r_YWt], [r_pbs])
                pba, r_pba = nextbank()
                for k in range(8):
                    MM(pba, wp[:, 1, k, eb * 128:(eb + 1) * 128], yab[:, k, :], k == 0, k == 7, [r_wp, r_yab], [r_pba])
                TT("dve", gs, pbs, gs, ALU.mult, [r_pbs, r_gs], [r_gs])
                TT("dve", ga, pba, ga, ALU.mult, [r_pba, r_ga], [r_ga])
                TT("pool", mg[:, eb, :], gs, ga, ALU.add, [r_gs, r_ga], [r_mg])
            for eb in range(8):
                pb, r_pb = nextbank()
                for k in range(8):
                    MM(pb, wp[:, 2, k, eb * 128:(eb + 1) * 128], mg[:, k, :], k == 0, k == 7, [r_wp, r_mg], [r_pb])
                STT("dve", xin[:, eb, :], pb, adac[:, 16 + eb:17 + eb], xin[:, eb, :], ALU.mult, ALU.add,
                    [r_pb, r_ada, r_xin], [r_xin])
            rms_rstd(nb, xin, r_xin)
            for eb in range(8):
                STT("dve", xin[:, eb, :], xin[:, eb, :], cols[:, C_FNW + eb:C_FNW + eb + 1],
                    nb["rstd"], ALU.mult, ALU.mult, [r_xin, r_const, nb["r_rstd"]], [r_xin])
            DMA("pool", outT.rearrange("(k p) t -> p k t", p=128)[:, :, ts_], xin, [r_xin], [], "outT")
        final = ["outT"]
    S.emit(final_waits=final)
    return nc

_O = {"z_ssm": (0, 1024), "xbc": (1024, 2304), "dt": (2304, 2320), "q": (2320, 3344), "k": (3344, 4368),
      "v": (4368, 5392), "z_att": (5392, 6416), "f": (6416, 6424), "g": (6424, 8472)}


def _colmajor(v, nblk):
    return np.ascontiguousarray(np.asarray(v, np.float32).reshape(nblk, 128).T)


def make_inputs(b, x, c, w_ada, b_ada, norm_w, w_in, conv_w, conv_b, dt_bias, a_log, d_skip,
                ssm_norm_w, b_f, b_gate, w_proj_ssm, w_proj_att, w_out, final_norm_w):
    f32 = np.float32
    w_in = np.asarray(w_in, f32)

    def cs(*names):
        return np.ascontiguousarray(np.concatenate([w_in[:, _O[n][0]:_O[n][1]] for n in names], axis=1))

    cols = np.zeros((128, NCOL), f32)
    cols[:, C_C:C_C + 8] = _colmajor(c[b], 8)
    cols[:, C_BADA:C_BADA + 24] = _colmajor(b_ada, 24)
    cols[:, C_NW:C_NW + 8] = _colmajor(norm_w, 8)
    cw = np.asarray(conv_w, f32).reshape(4, 10, 128).transpose(2, 1, 0).reshape(128, 40)
    cols[:, C_CW:C_CW + 40] = cw
    cols[:, C_CB:C_CB + 10] = _colmajor(conv_b, 10)
    cols[:, C_BG:C_BG + 16] = _colmajor(b_gate, 16)
    cols[:, C_FNW:C_FNW + 8] = _colmajor(final_norm_w, 8)
    cols[:64, C_GM] = 1.0
    cols[64:, C_GM + 1] = 1.0
    row = np.concatenate([np.asarray(dt_bias, f32), np.asarray(a_log, f32), np.asarray(b_f, f32),
                          np.repeat(np.asarray(d_skip, f32), 64), np.asarray(ssm_norm_w, f32)])
    rows = np.ascontiguousarray(np.broadcast_to(row[None, :], (128, NROW)))
    j = np.arange(128)
    U = (j[:, None] <= j[None, :]).astype(f32)
    SL = (j[:, None] > j[None, :]).astype(f32)
    half = np.broadcast_to((j[:, None] <= 63), (128, 128)).astype(f32)
    masks = np.ascontiguousarray(np.concatenate([U, SL, half, np.ones((128, 128), f32), np.eye(128, dtype=f32)], axis=1))
    return {
        "xT": np.ascontiguousarray(np.asarray(x[b], f32).T),
        "wA1": cs("q", "k", "v", "f"), "wA2": cs("xbc", "z_ssm", "dt"), "wC": cs("g", "z_att"),
        "wada": np.ascontiguousarray(np.asarray(w_ada, f32)),
        "wps": np.ascontiguousarray(np.asarray(w_proj_ssm, f32)),
        "wpa": np.ascontiguousarray(np.asarray(w_proj_att, f32)),
        "wout": np.ascontiguousarray(np.asarray(w_out, f32)),
        "cols": cols, "rows": rows, "masks": masks,
    }


_NC_CACHE = {}


def kernel(**inputs):
    inputs = {k: np.asarray(v) for k, v in inputs.items()}
    x = inputs["x"]
    B, S_TOK, D = x.shape
    if S_TOK not in _NC_CACHE:
        _NC_CACHE[S_TOK] = build(S_TOK)
    nc = _NC_CACHE[S_TOK]
    in_maps = [make_inputs(i // 2, **inputs) for i in range(8)]
    res = run_bass_kernel_spmd(nc, in_maps, core_ids=list(range(8)))
    out = np.empty((B, S_TOK, D), np.float32)
    half = S_TOK // 2
    for b in range(B):
        out[b, :half] = res.results[2 * b]["outT"][:, :half].T
        out[b, half:] = res.results[2 * b + 1]["outT"][:, half:].T
    return out
```

```python
import numpy as np
import concourse.bass as bass
import concourse.mybir as mybir
from concourse.bass_utils import run_bass_kernel_spmd

F32 = mybir.dt.float32
BF16 = mybir.dt.bfloat16
AF = mybir.ActivationFunctionType
ALU = mybir.AluOpType
AX = mybir.AxisListType


MUTE = [False]


class Res:
    __slots__ = ("w", "r", "name")

    def __init__(self, name=""):
        self.w = None
        self.r = {}
        self.name = name


class Op:
    __slots__ = ("eng", "fn", "deps", "idx", "dma_sem", "dma_cnt", "need_inc", "inc_cnt")


class Sched:
    ENGS = ("pe", "act", "dve", "pool", "sp")

    def __init__(self, nc):
        self.nc = nc
        self.ops = []
        self.per_eng = {e: [] for e in self.ENGS}
        self.dma_sems = {}
        self.bar = {}
        self.last = {}

    def _dep(self, deps, op):
        if op is None:
            return
        if op.dma_sem is not None:
            k = ("d", op.dma_sem)
            if deps.get(k, 0) < op.dma_cnt:
                deps[k] = op.dma_cnt
        else:
            k = ("e", op.eng)
            if k not in deps or deps[k].idx < op.idx:
                deps[k] = op

    def add(self, eng, fn, reads=(), writes=(), dma=None):
        if MUTE[0]:
            return None
        op = Op()
        op.eng = eng
        op.fn = fn
        op.idx = len(self.per_eng[eng])
        op.need_inc = False
        op.inc_cnt = 0
        op.dma_sem = None
        op.dma_cnt = 0
        deps = dict(self.bar)
        for r in reads:
            self._dep(deps, r.w)
        for w in writes:
            self._dep(deps, w.w)
            for ro in w.r.values():
                self._dep(deps, ro)
        if dma is not None:
            ent = self.dma_sems.setdefault(dma, [None, 0])
            ent[1] += 16
            op.dma_sem = dma
            op.dma_cnt = ent[1]
        if eng == "pe":
            deps.pop(("e", "pe"), None)
        op.deps = deps
        for k, d in deps.items():
            if k[0] == "e":
                d.need_inc = True
        rk = ("d", dma) if dma is not None else eng
        for r in reads:
            r.r[rk] = op
        for w in writes:
            w.w = op
            w.r = {}
        self.per_eng[eng].append(op)
        self.ops.append(op)
        if dma is None:
            self.last[eng] = op
        return op

    def barrier(self):
        bar = {}
        for e, op in self.last.items():
            if op.dma_sem is None:
                bar[("e", e)] = op
                op.need_inc = True
        for k, ent in self.dma_sems.items():
            bar[("d", k)] = ent[1]
        self.bar = bar

    def emit(self, final_waits=()):
        nc = self.nc
        sems = {e: nc.alloc_semaphore(name="sem_" + e) for e in self.ENGS}
        for k, ent in self.dma_sems.items():
            ent[0] = nc.alloc_semaphore(name="dsem_%s" % (k,))
        for e in self.ENGS:
            c = 0
            for op in self.per_eng[e]:
                if op.need_inc:
                    c += 1
                    op.inc_cnt = c
        per_eng = self.per_eng
        dma_sems = self.dma_sems

        def run(e, engobj):
            seen = {}
            for op in per_eng[e]:
                for k, d in op.deps.items():
                    if k[0] == "d":
                        sem, val = dma_sems[k[1]][0], d
                    else:
                        sem, val = sems[k[1]], d.inc_cnt
                    if seen.get(k, 0) >= val:
                        continue
                    seen[k] = val
                    engobj.wait_ge(sem, val)
                ins = op.fn(engobj)
                if op.dma_sem is not None:
                    ins.then_inc(dma_sems[op.dma_sem][0], 16)
                elif op.need_inc:
                    ins.then_inc(sems[e], 1)
            if e == "sp":
                for k in final_waits:
                    engobj.wait_ge(dma_sems[k][0], dma_sems[k][1])

        with nc.Block() as block:
            @block.tensor
            def _(eng):
                run("pe", eng)

            @block.scalar
            def _(eng):
                run("act", eng)

            @block.vector
            def _(eng):
                run("dve", eng)

            @block.gpsimd
            def _(eng):
                run("pool", eng)

            @block.sync
            def _(eng):
                run("sp", eng)


class Arena:
    def __init__(self, nc, words=51200):
        self.nc = nc
        self.big = nc.alloc_sbuf_tensor("arena", [128, words], F32)
        self.off = 0
        self.limit = words

    def mark(self):
        return self.off

    def reset(self, m):
        self.off = m

    def alloc(self, shape, dtype, name=None):
        n = int(np.prod(shape[1:]))
        words = n if dtype == F32 else (n + 1) // 2
        words = (words + 15) // 16 * 16
        assert self.off + words <= self.limit, ("SBUF overflow", name, self.off, words)
        v = self.big[:, self.off:self.off + words]
        self.off += words
        if dtype != F32:
            v = v.bitcast(dtype)
        v = v[:, 0:n]
        if len(shape) == 3:
            v = v.rearrange("p (a b) -> p a b", b=shape[2])
        elif len(shape) == 4:
            v = v.rearrange("p (a b c) -> p a b c", b=shape[2], c=shape[3])
        return v


class Psum:
    def __init__(self, nc):
        self.t = nc.alloc_psum_tensor("psum_all", [128, 4096], F32)

    def bank(self, b, n=1):
        return self.t[:, 512 * b:512 * (b + n)]

EPS = 1e-6
SCALE = 128 ** -0.5

C_C, C_BADA, C_NW, C_CW, C_CB, C_BG, C_FNW, C_GM, NCOL = 0, 8, 32, 40, 80, 90, 106, 114, 116
R_DTB, R_ALOG, R_BF, R_D, R_SW, NROW = 0, 16, 32, 40, 1064, 2088


DBG = [0]


def dbg(n):
    if DBG[0] == n:
        MUTE[0] = True


def build(S_TOK, phases="12BC"):
    NT = S_TOK // 512
    NB = S_TOK // 128
    BQ0 = 0
    nc = bass.Bass("TRN2", target_bir_lowering=False)

    def din(name, shape, dt=F32):
        return nc.dram_tensor(name, shape, dt, kind="ExternalInput").ap()

    xT = din("xT", [1024, S_TOK])
    wA1 = din("wA1", [1024, 3080])
    wA2 = din("wA2", [1024, 2320])
    wC = din("wC", [1024, 3072])
    wada = din("wada", [1024, 3072])
    wps = din("wps", [1024, 1024])
    wpa = din("wpa", [1024, 1024])
    wout = din("wout", [1024, 1024])
    cols_d = din("cols", [128, NCOL])
    rows_d = din("rows", [128, NROW])
    masks_d = din("masks", [128, 640])
    outT = nc.dram_tensor("outT", [1024, S_TOK], F32, kind="ExternalOutput").ap()
    KT = nc.dram_tensor("KT", [8, 128, S_TOK], BF16, kind="Internal").ap()
    QT = nc.dram_tensor("QT", [8, 128, S_TOK], BF16, kind="Internal").ap()
    VS = nc.dram_tensor("VS", [S_TOK, 1024], BF16, kind="Internal").ap()
    YA = nc.dram_tensor("YA", [1024, S_TOK], BF16, kind="Internal").ap()
    YW = nc.dram_tensor("YW", [1024, S_TOK], BF16, kind="Internal").ap()

    S = Sched(nc)
    A = Arena(nc)
    PS = Psum(nc)
    pbank = [(PS.bank(b), Res("pb%d" % b)) for b in range(8)]
    pb_rr = [0]

    def nextbank():
        b = pbank[pb_rr[0] % 8]
        pb_rr[0] += 1
        return b

    def MM(out, lhsT, rhs, st, sp, R, W):
        S.add("pe", lambda e: e.matmul(out, lhsT=lhsT, rhs=rhs, start=st, stop=sp), R, W)

    def TR(out, in_, ident, R, W):
        S.add("pe", lambda e: e.transpose(out, in_, ident), R, W)

    def ACT(out, in_, func, R, W, bias=None, scale=None, accum=None):
        kw = {}
        if bias is not None:
            kw["bias"] = bias
        if scale is not None:
            kw["scale"] = scale
        if accum is not None:
            kw["accum_out"] = accum
        S.add("act", lambda e: e.activation(out=out, in_=in_, func=func, **kw), R, W)

    def TT(eng, out, in0, in1, op, R, W):
        S.add(eng, lambda e: e.tensor_tensor(out=out, in0=in0, in1=in1, op=op), R, W)

    def TS(eng, out, in0, s1, s2, op0, op1, R, W):
        if op1 is None:
            S.add(eng, lambda e: e.tensor_scalar(out=out, in0=in0, scalar1=s1, scalar2=None, op0=op0), R, W)
        else:
            S.add(eng, lambda e: e.tensor_scalar(out=out, in0=in0, scalar1=s1, scalar2=s2, op0=op0, op1=op1), R, W)

    def STT(eng, out, in0, scalar, in1, op0, op1, R, W):
        S.add(eng, lambda e: e.scalar_tensor_tensor(out=out, in0=in0, scalar=scalar, in1=in1, op0=op0, op1=op1), R, W)

    def CP(eng, out, in_, R, W):
        if eng == "act":
            ACT(out, in_, AF.Copy, R, W)
        else:
            S.add(eng, lambda e: e.tensor_copy(out=out, in_=in_), R, W)

    def MEMSET(eng, ap, val, W):
        S.add(eng, lambda e: e.memset(ap, val), [], W)

    def DMA(q, out, in_, R, W, key):
        S.add(q, lambda e: e.dma_start(out=out, in_=in_), R, W, dma=key)

    cols = A.alloc([128, NCOL], F32, "cols")
    rows = A.alloc([128, NROW], F32, "rows")
    mk = A.alloc([128, 640], F32, "masks")
    r_const = Res("const")
    DMA("sp", cols, cols_d, [], [r_const], "c0")
    DMA("sp", rows, rows_d, [], [r_const], "c1")
    DMA("sp", mk, masks_d, [], [r_const], "c2")
    U_f, SL_f, half_f, ones_f, id_f = (mk[:, 128 * i:128 * (i + 1)] for i in range(5))
    mkb = A.alloc([128, 384], BF16, "masksb")
    r_cb = Res("constb")
    CP("dve", mkb[:, 0:128], U_f, [r_const], [r_cb])
    CP("dve", mkb[:, 128:256], ones_f, [r_const], [r_cb])
    CP("dve", mkb[:, 256:384], id_f, [r_const], [r_cb])
    U_b, ones_b, id_b = mkb[:, 0:128], mkb[:, 128:256], mkb[:, 256:384]
    adac = A.alloc([128, 32], F32, "adac")
    r_ada = Res("ada")
    aneg = A.alloc([128, 16], F32, "aneg")
    Ftok = A.alloc([128, NB, 8], F32, "Ftok")
    Cend = A.alloc([128, NB, 8], F32, "Cend")
    Cmid = A.alloc([128, NB, 8], F32, "Cmid")
    nCend = A.alloc([128, NB, 8], F32, "nCend")
    zero8 = A.alloc([128, 8], F32, "zero8")
    r_F = Res("F")
    MEMSET("pool", zero8, 0.0, [r_F])
    stage = [(A.alloc([128, 1024], F32, "stage%d" % i), Res("stage%d" % i)) for i in range(2)]
    st_rr = [0]
    base_mark = A.mark()

    def load_w(dst, r_dst, src, ncols):
        for k in range(8):
            c0 = 0
            while c0 < ncols:
                n = min(1024, ncols - c0)
                stg, r_s = stage[st_rr[0] % 2]
                key = "stage%d" % (st_rr[0] % 2)
                st_rr[0] += 1
                DMA("sp", stg[:, 0:n], src[k * 128:(k + 1) * 128, c0:c0 + n], [], [r_s], key)
                CP("pool" if (st_rr[0] % 2) else "dve", dst[:, k, c0:c0 + n], stg[:, 0:n], [r_s], [r_dst])
                c0 += n

    m0 = A.mark()
    wada_sb = A.alloc([128, 8, 3072], F32, "wada")
    r_wada = Res("wada")
    for k in range(8):
        DMA("sp", wada_sb[:, k, :], wada[k * 128:(k + 1) * 128, :], [], [r_wada], "wada")
    sc = A.alloc([128, 8], F32, "silu_c")
    r_sc = Res("sc")
    ACT(sc, cols[:, C_C:C_C + 8], AF.Silu, [r_const], [r_sc])
    pb, r_pb = nextbank()
    for eb in range(24):
        for k in range(8):
            MM(pb[:, eb:eb + 1], wada_sb[:, k, eb * 128:(eb + 1) * 128], sc[:, k:k + 1], k == 0, k == 7,
               [r_wada, r_sc], [r_pb])
    adaf = A.alloc([128, 24], F32, "adaf")
    TT("dve", adaf, pb[:, 0:24], cols[:, C_BADA:C_BADA + 24], ALU.add, [r_pb, r_const], [r_ada])
    TS("dve", adac[:, 24:32], adaf[:, 8:16], 1.0, None, ALU.add, None, [r_ada], [r_ada])
    TT("dve", adac[:, 0:8], adac[:, 24:32], cols[:, C_NW:C_NW + 8], ALU.mult, [r_ada, r_const], [r_ada])
    CP("dve", adac[:, 8:16], adaf[:, 0:8], [r_ada], [r_ada])
    CP("dve", adac[:, 16:24], adaf[:, 16:24], [r_ada], [r_ada])
    ACT(aneg, rows[:, R_ALOG:R_ALOG + 16], AF.Exp, [r_const], [r_ada])
    TS("dve", aneg, aneg, -1.0, None, ALU.mult, None, [r_ada], [r_ada])
    S.barrier()
    A.reset(m0)

    def norm_bufs():
        d = {}
        d["xin"] = A.alloc([128, 8, 512], F32, "xin"); d["r_xin"] = Res("xin")
        d["sq"] = A.alloc([128, 8, 512], BF16, "sq"); d["r_sq"] = Res("sq")
        d["hT"] = A.alloc([128, 8, 512], BF16, "hT"); d["r_hT"] = Res("hT")
        d["tmp"] = [(A.alloc([128, 512], F32, "tmp"), Res("tmp")) for _ in range(2)]
        d["rstd"] = A.alloc([128, 512], F32, "rstd"); d["r_rstd"] = Res("rstd")
        return d

    def rms_rstd(nb, src, r_src):
        for k in range(8):
            ACT(nb["sq"][:, k, :], src[:, k, :], AF.Square, [r_src], [nb["r_sq"]])
        pb, r_pb = nextbank()
        for k in range(8):
            MM(pb, ones_b, nb["sq"][:, k, :], k == 0, k == 7, [nb["r_sq"], r_cb], [r_pb])
        TS("dve", nb["rstd"], pb, 1.0 / 1024, EPS, ALU.mult, ALU.add, [r_pb], [nb["r_rstd"]])
        ACT(nb["rstd"], nb["rstd"], AF.Ln, [nb["r_rstd"]], [nb["r_rstd"]])
        ACT(nb["rstd"], nb["rstd"], AF.Exp, [nb["r_rstd"]], [nb["r_rstd"]], scale=-0.5)

    def load_x(nb, i):
        DMA("sp", nb["xin"], xT.rearrange("(k p) t -> p k t", p=128)[:, :, i * 512:(i + 1) * 512],
            [], [nb["r_xin"]], "xin")

    def norm_tile(nb):
        rms_rstd(nb, nb["xin"], nb["r_xin"])
        for k in range(8):
            tmp, r_tmp = nb["tmp"][k % 2]
            TT("dve", tmp, nb["xin"][:, k, :], nb["rstd"], ALU.mult, [nb["r_xin"], nb["r_rstd"]], [r_tmp])
            ACT(nb["hT"][:, k, :], tmp, AF.Identity, [r_tmp, r_ada], [nb["r_hT"]],
                bias=adac[:, 8 + k:9 + k], scale=adac[:, k:k + 1])

    ev_rr = [0]

    def evac(out, in_, R, W):
        e = "act" if ev_rr[0] % 2 == 0 else "dve"
        ev_rr[0] += 1
        CP(e, out, in_, R, W)

    if "1" in phases:
        m1 = A.mark()
        w1 = A.alloc([128, 8, 3080], BF16, "w1"); r_w1 = Res("w1")
        load_w(w1, r_w1, wA1, 3080)
        nb = norm_bufs()
        kq = A.alloc([128, 16, 512], BF16, "kq"); r_kq = Res("kq")
        vt = [(A.alloc([128, 1024], BF16, "vt"), Res("vt")) for _ in range(2)]
        fsm = A.alloc([128, 32], F32, "fsm"); r_fsm = Res("fsm")
        for i in range(NT):
            load_x(nb, i)
            norm_tile(nb)
            hT, r_hT = nb["hT"], nb["r_hT"]
            for blk in range(16):
                pb, r_pb = nextbank()
                for k in range(8):
                    MM(pb, w1[:, k, blk * 128:(blk + 1) * 128], hT[:, k, :], k == 0, k == 7, [r_w1, r_hT], [r_pb])
                evac(kq[:, blk, :], pb, [r_pb], [r_kq])
            DMA("pool", QT.rearrange("h d t -> d h t")[:, :, i * 512:(i + 1) * 512], kq[:, 0:8, :], [r_kq], [], "kq")
            DMA("pool", KT.rearrange("h d t -> d h t")[:, :, i * 512:(i + 1) * 512], kq[:, 8:16, :], [r_kq], [], "kq")
            for c in range(4):
                blk = i * 4 + c
                tsl = slice(c * 128, (c + 1) * 128)
                v_sb, r_v = vt[c % 2]
                for hf in range(2):
                    pb, r_pb = nextbank()
                    for k in range(8):
                        MM(pb, hT[:, k, tsl], w1[:, k, 2048 + hf * 512:2048 + (hf + 1) * 512], k == 0, k == 7,
                           [r_w1, r_hT], [r_pb])
                    evac(v_sb[:, hf * 512:(hf + 1) * 512], pb, [r_pb], [r_v])
                DMA("pool", VS[blk * 128:(blk + 1) * 128, :], v_sb, [r_v], [], "vt%d" % (c % 2))
                pb, r_pb = nextbank()
                for k in range(8):
                    MM(pb[:, 0:8], hT[:, k, tsl], w1[:, k, 3072:3080], k == 0, k == 7, [r_w1, r_hT], [r_pb])
                TT("dve", fsm[:, 0:8], pb[:, 0:8], rows[:, R_BF:R_BF + 8], ALU.add, [r_pb, r_const], [r_fsm])
                ACT(fsm[:, 8:16], fsm[:, 0:8], AF.Exp, [r_fsm], [r_fsm], scale=-1.0)
                ACT(fsm[:, 16:24], fsm[:, 8:16], AF.Ln, [r_fsm], [r_fsm], bias=1.0)
                TS("dve", fsm[:, 24:32], fsm[:, 16:24], -1.0, None, ALU.mult, None, [r_fsm], [r_fsm])
                lf = fsm[:, 24:32]
                pb, r_pb = nextbank()
                MM(pb[:, 0:8], U_f, lf, True, True, [r_fsm, r_const], [r_pb])
                MM(pb[:, 8:16], ones_f, lf, True, True, [r_fsm, r_const], [r_pb])
                MM(pb[:, 16:24], half_f, lf, True, True, [r_fsm, r_const], [r_pb])
                carry = zero8 if blk == 0 else Cend[:, blk - 1, :]
                TT("dve", Ftok[:, blk, :], pb[:, 0:8], carry, ALU.add, [r_pb, r_F], [r_F])
                TT("dve", Cmid[:, blk, :], pb[:, 16:24], carry, ALU.add, [r_pb, r_F], [r_F])
                TT("dve", Cend[:, blk, :], pb[:, 8:16], carry, ALU.add, [r_pb, r_F], [r_F])
                TS("dve", nCend[:, blk, :], Cend[:, blk, :], -1.0, None, ALU.mult, None, [r_F], [r_F])
        S.barrier()
        A.reset(m1)

    if "2" in phases:
        m1 = A.mark()
        w2 = A.alloc([128, 8, 2320], BF16, "w2"); r_w2 = Res("w2")
        load_w(w2, r_w2, wA2, 2320)
        nb = norm_bufs()
        hT, r_hT = nb["hT"], nb["r_hT"]
        ubuf = A.alloc([128, 10, 515], F32, "ubuf"); r_u = Res("ubuf")
        MEMSET("pool", ubuf[:, :, 0:3], 0.0, [r_u])
        cacc = [(A.alloc([128, 512], F32, "cacc"), Res("cacc")) for _ in range(2)]
        xsT = A.alloc([128, 10, 512], BF16, "xsT"); r_xsT = Res("xsT")
        bcg = A.alloc([128, 4, 512], BF16, "bcg"); r_bcg = Res("bcg")
        xs_tok = A.alloc([128, 1024], BF16, "xs_tok"); r_xst = Res("xs_tok")
        B_tok = A.alloc([128, 128], BF16, "B_tok"); r_Bt = Res("B_tok")
        sm = A.alloc([128, 160], F32, "sm"); r_sm = Res("sm")
        zs = A.alloc([128, 1024], BF16, "zs"); r_zs = Res("zs")
        Rb = A.alloc([128, 16, 128], F32, "Rb"); r_R = Res("R")
        cbm = A.alloc([128, 2, 128], F32, "cbm"); r_cbm = Res("cbm")
        eseg = [(A.alloc([128, 512], F32, "eseg"), Res("eseg")) for _ in range(2)]
        MT = A.alloc([128, 16, 128], BF16, "MT"); r_MT = Res("MT")
        xdt = A.alloc([128, 1024], BF16, "xdt"); r_xdt = Res("xdt")
        xdtd = A.alloc([128, 1024], BF16, "xdtd"); r_xdtd = Res("xdtd")
        Sst = A.alloc([128, 512], F32, "Sst"); r_S = Res("Sst")
        Sbf = A.alloc([128, 512], BF16, "Sbf"); r_Sb = Res("Sbf")
        y1 = A.alloc([128, 1024], F32, "y1"); r_y1 = Res("y1")
        y2 = A.alloc([128, 1024], F32, "y2"); r_y2 = Res("y2")
        junk = A.alloc([128, 1024], BF16, "junk"); r_junk = Res("junk")
        Yn = A.alloc([128, 1024], BF16, "Yn"); r_Yn = Res("Yn")
        YwT = A.alloc([128, 8, 512], BF16, "YwT"); r_YwT = Res("YwT")
        MEMSET("pool", Sst, 0.0, [r_S])
        MEMSET("pool", Sbf, 0.0, [r_Sb])
        dtraw, dtx, dt, a_t, acs, atot, eacs, dd, dec, eatot, ssq = (
            sm[:, 0:16], sm[:, 16:32], sm[:, 32:48], sm[:, 48:64], sm[:, 64:80], sm[:, 80:96],
            sm[:, 96:112], sm[:, 112:128], sm[:, 128:144], sm[:, 144:160], None)
        sm2 = A.alloc([128, 16], F32, "sm2"); r_sm2 = Res("sm2")
        for i in range(NT if DBG[0] == 0 else 1):
            load_x(nb, i)
            norm_tile(nb)
            dbg(1)
            if i > 0:
                CP("pool", ubuf[:, :, 0:3], ubuf[:, :, 512:515], [r_u], [r_u])
            for blk in range(10):
                pb, r_pb = nextbank()
                for k in range(8):
                    MM(pb, w2[:, k, blk * 128:(blk + 1) * 128], hT[:, k, :], k == 0, k == 7, [r_w2, r_hT], [r_pb])
                evac(ubuf[:, blk, 3:515], pb, [r_pb], [r_u])
            dbg(2)
            for blk in range(10):
                ca, r_ca = cacc[blk % 2]
                cw = C_CW + blk * 4
                TS("dve", ca, ubuf[:, blk, 0:512], cols[:, cw:cw + 1], cols[:, C_CB + blk:C_CB + blk + 1],
                   ALU.mult, ALU.add, [r_u, r_const], [r_ca])
                for tap in range(1, 4):
                    STT("dve", ca, ubuf[:, blk, tap:tap + 512], cols[:, cw + tap:cw + tap + 1], ca,
                        ALU.mult, ALU.add, [r_u, r_const, r_ca], [r_ca])
                ACT(xsT[:, blk, :], ca, AF.Silu, [r_ca], [r_xsT])
            for g in range(2):
                TS("pool", bcg[:, g, :], xsT[:, 8, :], cols[:, C_GM + g:C_GM + g + 1], None, ALU.mult, None,
                   [r_xsT, r_const], [r_bcg])
                TS("pool", bcg[:, 2 + g, :], xsT[:, 9, :], cols[:, C_GM + g:C_GM + g + 1], None, ALU.mult, None,
                   [r_xsT, r_const], [r_bcg])
            dbg(3)
            for c in range(4):
                tsl = slice(c * 128, (c + 1) * 128)
                for hf in range(2):
                    pb, r_pb = nextbank()
                    for k in range(8):
                        MM(pb, hT[:, k, tsl], w2[:, k, 1280 + hf * 512:1280 + (hf + 1) * 512], k == 0, k == 7,
                           [r_w2, r_hT], [r_pb])
                    ACT(zs[:, hf * 512:(hf + 1) * 512], pb, AF.Silu, [r_pb], [r_zs])
                pb, r_pb = nextbank()
                for k in range(8):
                    MM(pb[:, 0:16], hT[:, k, tsl], w2[:, k, 2304:2320], k == 0, k == 7, [r_w2, r_hT], [r_pb])
                TT("dve", dtraw, pb[:, 0:16], rows[:, R_DTB:R_DTB + 16], ALU.add, [r_pb, r_const], [r_sm])
                ACT(dtx, dtraw, AF.Exp, [r_sm], [r_sm])
                ACT(dt, dtx, AF.Ln, [r_sm], [r_sm], bias=1.0)
                TT("dve", a_t, dt, aneg, ALU.mult, [r_sm, r_ada], [r_sm])
                dbg(4)
                pbt, r_pbt = nextbank()
                pbt_b = pbt.bitcast(BF16)
                for blk in range(8):
                    TR(pbt_b[:, blk * 128:(blk + 1) * 128], xsT[:, blk, tsl], id_b, [r_xsT, r_cb], [r_pbt])
                evac(xs_tok, pbt_b, [r_pbt], [r_xst])
                pbt, r_pbt = nextbank()
                pbt_b = pbt.bitcast(BF16)
                TR(pbt_b[:, 0:128], xsT[:, 8, tsl], id_b, [r_xsT, r_cb], [r_pbt])
                evac(B_tok, pbt_b[:, 0:128], [r_pbt], [r_Bt])
                dbg(5)
                pb, r_pb = nextbank()
                MM(pb[:, 0:16], U_f, a_t, True, True, [r_sm, r_const], [r_pb])
                MM(pb[:, 16:32], ones_f, a_t, True, True, [r_sm, r_const], [r_pb])
                CP("dve", sm[:, 64:96], pb[:, 0:32], [r_pb], [r_sm])
                ACT(eacs, acs, AF.Exp, [r_sm], [r_sm])
                TT("dve", dd, atot, acs, ALU.subtract, [r_sm], [r_sm])
                ACT(dec, dd, AF.Exp, [r_sm], [r_sm])
                ACT(eatot, atot, AF.Exp, [r_sm], [r_sm])
                dbg(6)
                TT("dve", Rb, U_f.unsqueeze(1).to_broadcast([128, 16, 128]),
                   a_t.unsqueeze(2).to_broadcast([128, 16, 128]), ALU.mult, [r_sm, r_const], [r_R])
                dbg(61)
                pb, r_pb = nextbank()
                for g in range(2):
                    MM(pb[:, g * 128:(g + 1) * 128], bcg[:, g, tsl], xsT[:, 9, tsl],
                       True, True, [r_xsT, r_bcg], [r_pb])
                dbg(62)
                TT("dve", cbm, pb[:, 0:256].rearrange("p (g l) -> p g l", g=2),
                   U_f.unsqueeze(1).to_broadcast([128, 2, 128]), ALU.mult, [r_pb, r_const], [r_cbm])
                dbg(7)
                for q4 in range(4):
                    pb, r_pb = nextbank()
                    MM(pb, SL_f, Rb[:, 4 * q4:4 * q4 + 4, :].rearrange("p h l -> p (h l)"), True, True,
                       [r_R, r_const], [r_pb])
                    es, r_es = eseg[q4 % 2]
                    ACT(es, pb, AF.Exp, [r_pb], [r_es])
                    g = q4 // 2
                    TT("dve", MT[:, 4 * q4:4 * q4 + 4, :], es.rearrange("p (h l) -> p h l", h=4),
                       cbm[:, g:g + 1, :].to_broadcast([128, 4, 128]), ALU.mult, [r_es, r_cbm], [r_MT])
                dbg(8)
                TT("dve", xdt.rearrange("p (h d) -> p h d", h=16), xs_tok.rearrange("p (h d) -> p h d", h=16),
                   dt.unsqueeze(2).to_broadcast([128, 16, 64]), ALU.mult, [r_xst, r_sm], [r_xdt])
                TT("pool", xdtd.rearrange("p (h d) -> p h d", h=16), xdt.rearrange("p (h d) -> p h d", h=16),
                   dec.unsqueeze(2).to_broadcast([128, 16, 64]), ALU.mult, [r_xdt, r_sm], [r_xdtd])
                dbg(9)
                yd = []
                for hf in range(2):
                    pb, r_pb = nextbank()
                    for hh in range(8):
                        h = hf * 8 + hh
                        MM(pb[:, hh * 64:(hh + 1) * 64], MT[:, h, :], xdt[:, h * 64:(h + 1) * 64], True, True,
                           [r_MT, r_xdt], [r_pb])
                    yd.append((pb, r_pb))
                yo = []
                for g in range(2):
                    pb, r_pb = nextbank()
                    MM(pb, bcg[:, 2 + g, tsl], Sbf, True, True, [r_bcg, r_Sb], [r_pb])
                    yo.append((pb, r_pb))
                dbg(10)
                for g in range(2):
                    sl = slice(g * 512, (g + 1) * 512)
                    TT("dve", y1[:, sl].rearrange("p (h d) -> p h d", h=8),
                       yo[g][0].rearrange("p (h d) -> p h d", h=8),
                       eacs[:, 8 * g:8 * g + 8].unsqueeze(2).to_broadcast([128, 8, 64]), ALU.mult,
                       [yo[g][1], r_sm], [r_y1])
                    TT("dve", y1[:, sl], yd[g][0], y1[:, sl], ALU.add, [yd[g][1], r_y1], [r_y1])
                TT("pool", y2, xs_tok, rows[:, R_D:R_D + 1024], ALU.mult, [r_xst, r_const], [r_y2])
                TT("pool", y1, y1, y2, ALU.add, [r_y1, r_y2], [r_y1])
                TT("dve", y2, y1, zs, ALU.mult, [r_y1, r_zs], [r_y2])
                ACT(junk, y2, AF.Square, [r_y2], [r_junk, r_sm2], accum=sm2[:, 0:1])
                TS("dve", sm2[:, 1:2], sm2[:, 0:1], 1.0 / 1024, EPS, ALU.mult, ALU.add, [r_sm2], [r_sm2])
                ACT(sm2[:, 2:3], sm2[:, 1:2], AF.Ln, [r_sm2], [r_sm2])
                ACT(sm2[:, 3:4], sm2[:, 2:3], AF.Exp, [r_sm2], [r_sm2], scale=-0.5)
                STT("dve", Yn, y2, sm2[:, 3:4], rows[:, R_SW:R_SW + 1024], ALU.mult, ALU.mult,
                    [r_y2, r_sm2, r_const], [r_Yn])
                dbg(11)
                pbt, r_pbt = nextbank()
                pbt_b = pbt.bitcast(BF16)
                for blk in range(8):
                    TR(pbt_b[:, blk * 128:(blk + 1) * 128], Yn[:, blk * 128:(blk + 1) * 128], id_b, [r_Yn, r_cb], [r_pbt])
                evac(YwT[:, :, tsl], pbt_b.rearrange("p (b t) -> p b t", b=8), [r_pbt], [r_YwT])
                dbg(12)
                for g in range(2):
                    pb, r_pb = nextbank()
                    MM(pb, B_tok, xdtd[:, g * 512:(g + 1) * 512], True, True, [r_Bt, r_xdtd], [r_pb])
                    ps_ = slice(64 * g, 64 * g + 64)
                    TT("dve", Sst[ps_, :].rearrange("p (h d) -> p h d", h=8),
                       Sst[ps_, :].rearrange("p (h d) -> p h d", h=8),
                       eatot[ps_, 8 * g:8 * g + 8].unsqueeze(2).to_broadcast([64, 8, 64]), ALU.mult,
                       [r_S, r_sm], [r_S])
                    TT("dve", Sst[ps_, :], pb[ps_, :], Sst[ps_, :], ALU.add, [r_pb, r_S], [r_S])
                    CP("pool", Sbf[ps_, :], Sst[ps_, :], [r_S], [r_Sb])
            DMA("pool", YW.rearrange("(k p) t -> p k t", p=128)[:, :, i * 512:(i + 1) * 512], YwT, [r_YwT], [], "YwT")
        S.barrier()
        A.reset(m1)

    if "B" in phases:
        m1 = A.mark()
        hb = []
        for s_ in range(2):
            hb.append(dict(K=A.alloc([128, S_TOK], BF16, "Kh"), V=A.alloc([128, NB, 128], BF16, "Vh"),
                           Q=A.alloc([128, S_TOK], BF16, "Qh"), r=Res("hb%d" % s_)))
        bfar = [(A.alloc([128, NB], F32, "bfar"), Res("bfar")) for _ in range(2)]
        bnear = [(A.alloc([128, 4, 4], F32, "bnear"), Res("bnear")) for _ in range(2)]
        fac = [(A.alloc([128, 4], F32, "fac"), Res("fac")) for _ in range(2)]
        pTf = [(A.alloc([128, 512], BF16, "pTf"), Res("pTf")) for _ in range(4)]
        pTn = [[(A.alloc([128, 512], BF16, "pTn"), Res("pTn")) for _ in range(4)] for _ in range(2)]
        for st_ in range(2):
            for kb in range(1, 4):
                MEMSET("pool", pTn[st_][kb][0][:, 0:128 * kb], 0.0, [pTn[st_][kb][1]])
        onb = [(A.alloc([128, 512], F32, "on_sb"), Res("on")) for _ in range(2)]
        dnb = [(A.alloc([128, 512], F32, "dn_sb"), Res("dn")) for _ in range(2)]
        yat = [(A.alloc([128, 512], BF16, "yat"), Res("yat")) for _ in range(2)]
        st_b = [pbank[0], pbank[1], pbank[2]]
        ofar, r_ofar = pbank[3]
        dfar, r_dfar = pbank[4]
        onear, r_onear = pbank[5]
        dnear, r_dnear = pbank[6]

        def load_head(h):
            d = hb[h % 2]
            key = "hb%d" % (h % 2)
            DMA("sp", d["K"], KT[h], [], [d["r"]], key)
            DMA("sp", d["Q"], QT[h], [], [d["r"]], key)
            vsrc = VS.rearrange("(b p) (h d) -> p b h d", p=128, h=8)
            nstep = max(1, NB // 8)
            for b0 in range(0, NB, nstep):
                DMA("sp", d["V"][:, b0:b0 + nstep, :], vsrc[:, b0:b0 + nstep, h, :], [], [d["r"]], key)

        jobs = []
        tix = 0
        for h in range(8):
            for i in range(BQ0, NT):
                nfar = 4 * i
                for c in range(nfar):
                    jobs.append(dict(h=h, i=i, far=True, c=c, kb=0, first=(c == 0), last=False, tix=tix, nfar=nfar))
                for kb in range(4):
                    jobs.append(dict(h=h, i=i, far=False, c=nfar + kb, kb=kb, first=(nfar == 0 and kb == 0),
                                     last=(kb == 3), tix=tix, nfar=nfar))
                tix += 1

        def prep(j):
            h, i, nfar, p = j["h"], j["i"], j["nfar"], j["tix"] % 2
            if i == BQ0 and h == 0:
                load_head(0)
                load_head(1)
            if nfar > 0:
                TS("dve", bfar[p][0][:, 0:nfar], Ftok[:, 0:nfar, h], -1.0, Cend[:, nfar - 1, h:h + 1],
                   ALU.mult, ALU.add, [r_F], [bfar[p][1]])
                ACT(fac[p][0], Cmid[:, nfar:nfar + 4, h], AF.Exp, [r_F], [fac[p][1]],
                    bias=nCend[:, nfar - 1, h:h + 1])
            TT("dve", bnear[p][0], Cmid[:, nfar:nfar + 4, h].unsqueeze(1).to_broadcast([128, 4, 4]),
               Ftok[:, nfar:nfar + 4, h].unsqueeze(2).to_broadcast([128, 4, 4]), ALU.subtract,
               [r_F], [bnear[p][1]])

        def rec_qk(n):
            j = jobs[n]
            if j["first"]:
                prep(j)
            d = hb[j["h"] % 2]
            stp, r_st = st_b[n % 3]
            c = j["c"]
            qs = slice(j["i"] * 512, j["i"] * 512 + 512)
            MM(stp, d["K"][:, c * 128:(c + 1) * 128], d["Q"][:, qs], True, True, [d["r"]], [r_st])

        pfc = [0]

        def rec_rest(n):
            j = jobs[n]
            h, i, nfar, p, c = j["h"], j["i"], j["nfar"], j["tix"] % 2, j["c"]
            d = hb[h % 2]
            Vh, r_h = d["V"], d["r"]
            stp, r_st = st_b[n % 3]
            if j["far"]:
                pT, r_pT = pTf[pfc[0] % 4]
                pfc[0] += 1
                ACT(pT, stp, AF.Exp, [r_st, bfar[p][1]], [r_pT], bias=bfar[p][0][:, c:c + 1], scale=SCALE)
                MM(ofar, Vh[:, c, :], pT, c == 0, c == nfar - 1, [r_h, r_pT], [r_ofar])
                MM(dfar, ones_b, pT, c == 0, c == nfar - 1, [r_cb, r_pT], [r_dfar])
                return
            kb = j["kb"]
            pT, r_pT = pTn[p][kb]
            for jj in range(kb, 4):
                js = slice(128 * jj, 128 * (jj + 1))
                ACT(pT[:, js], stp[:, js], AF.Exp, [r_st, bnear[p][1]], [r_pT], bias=bnear[p][0][:, kb, jj:jj + 1],
                    scale=SCALE)
            ks = slice(128 * kb, 128 * (kb + 1))
            TT("pool", pT[:, ks], pT[:, ks], U_b, ALU.mult, [r_pT, r_cb], [r_pT])
            MM(onear, Vh[:, c, :], pT, kb == 0, kb == 3, [r_h, r_pT], [r_onear])
            MM(dnear, ones_b, pT, kb == 0, kb == 3, [r_cb, r_pT], [r_dnear])
            if not j["last"]:
                return
            on_sb, r_on = onb[p]
            dn_sb, r_dn = dnb[p]
            CP("dve", on_sb, onear, [r_onear], [r_on])
            CP("dve", dn_sb, dnear, [r_dnear], [r_dn])
            if nfar > 0:
                for jj in range(4):
                    js = slice(128 * jj, 128 * (jj + 1))
                    STT("dve", on_sb[:, js], ofar[:, js], fac[p][0][:, jj:jj + 1], on_sb[:, js], ALU.mult, ALU.add,
                        [r_ofar, fac[p][1], r_on], [r_on])
                    STT("dve", dn_sb[:, js], dfar[:, js], fac[p][0][:, jj:jj + 1], dn_sb[:, js], ALU.mult, ALU.add,
                        [r_dfar, fac[p][1], r_dn], [r_dn])
            S.add("dve", lambda e, o=dn_sb: e.reciprocal(out=o, in_=o), [r_dn], [r_dn])
            ya, r_ya = yat[p]
            TT("dve", ya, on_sb, dn_sb, ALU.mult, [r_on, r_dn], [r_ya])
            qs = slice(i * 512, i * 512 + 512)
            DMA("pool", YA[h * 128:(h + 1) * 128, qs], ya, [r_ya], [], "yat%d" % p)
            if i == NT - 1 and h + 2 < 8:
                load_head(h + 2)

        LOOK = 2
        for n in range(len(jobs) + LOOK):
            if n < len(jobs):
                rec_qk(n)
            if n - LOOK >= 0:
                rec_rest(n - LOOK)
        S.barrier()
        A.reset(m1)

    MUTE[0] = False
    final = []
    if "C" in phases:
        w3 = A.alloc([128, 8, 3072], BF16, "w3"); r_w3 = Res("w3")
        load_w(w3, r_w3, wC, 3072)
        wp = A.alloc([128, 3, 8, 1024], BF16, "wp"); r_wp = Res("wp")
        for j, src in enumerate((wps, wpa, wout)):
            load_w(wp[:, j], r_wp, src, 1024)
        nb = norm_bufs()
        hT, r_hT = nb["hT"], nb["r_hT"]
        xin, r_xin = nb["xin"], nb["r_xin"]
        YAt = A.alloc([128, 8, 512], BF16, "YAt"); r_YAt = Res("YAt")
        YWt = A.alloc([128, 8, 512], BF16, "YWt"); r_YWt = Res("YWt")
        yab = A.alloc([128, 8, 512], BF16, "yab"); r_yab = Res("yab")
        mg = A.alloc([128, 8, 512], BF16, "mg"); r_mg = Res("mg")
        t4 = [(A.alloc([128, 512], F32, "t4"), Res("t4")) for _ in range(2)]
        for i in range(NT):
            ts_ = slice(i * 512, (i + 1) * 512)
            load_x(nb, i)
            DMA("sp", YAt, YA.rearrange("(k p) t -> p k t", p=128)[:, :, ts_], [], [r_YAt], "YAt")
            DMA("sp", YWt, YW.rearrange("(k p) t -> p k t", p=128)[:, :, ts_], [], [r_YWt], "YWt")
            norm_tile(nb)
            for cb_ in range(8):
                pb, r_pb = nextbank()
                for k in range(8):
                    MM(pb, w3[:, k, 2048 + cb_ * 128:2048 + (cb_ + 1) * 128], hT[:, k, :], k == 0, k == 7,
                       [r_w3, r_hT], [r_pb])
                t, r_t = t4[cb_ % 2]
                ACT(t, pb, AF.Silu, [r_pb], [r_t])
                TT("dve", yab[:, cb_, :], YAt[:, cb_, :], t, ALU.mult, [r_YAt, r_t], [r_yab])
            for eb in range(8):
                gs, r_gs = t4[0]
                ga, r_ga = t4[1]
                for (gt, r_gt, off) in ((gs, r_gs, 0), (ga, r_ga, 1024)):
                    pb, r_pb = nextbank()
                    for k in range(8):
                        MM(pb, w3[:, k, off + eb * 128:off + (eb + 1) * 128], hT[:, k, :], k == 0, k == 7,
                           [r_w3, r_hT], [r_pb])
                    bcol = C_BG + (off // 128) + eb
                    ACT(gt, pb, AF.Sigmoid, [r_pb, r_const], [r_gt], bias=cols[:, bcol:bcol + 1])
                pbs, r_pbs = nextbank()
                for k in range(8):
                    MM(pbs, wp[:, 0, k, eb * 128:(eb + 1) * 128], YWt[:, k, :], k == 0, k == 7, [r_wp, r_YWt], [r_pbs])
                pba, r_pba = nextbank()
                for k in range(8):
                    MM(pba, wp[:, 1, k, eb * 128:(eb + 1) * 128], yab[:, k, :], k == 0, k == 7, [r_wp, r_yab], [r_pba])
                TT("dve", gs, pbs, gs, ALU.mult, [r_pbs, r_gs], [r_gs])
                TT("dve", ga, pba, ga, ALU.mult, [r_pba, r_ga], [r_ga])
                TT("pool", mg[:, eb, :], gs, ga, ALU.add, [r_gs, r_ga], [r_mg])
            for eb in range(8):
                pb, r_pb = nextbank()
                for k in range(8):
                    MM(pb, wp[:, 2, k, eb * 128:(eb + 1) * 128], mg[:, k, :], k == 0, k == 7, [r_wp, r_mg], [r_pb])
                STT("dve", xin[:, eb, :], pb, adac[:, 16 + eb:17 + eb], xin[:, eb, :], ALU.mult, ALU.add,
                    [r_pb, r_ada, r_xin], [r_xin])
            rms_rstd(nb, xin, r_xin)
            for eb in range(8):
                STT("dve", xin[:, eb, :], xin[:, eb, :], cols[:, C_FNW + eb:C_FNW + eb + 1],
                    nb["rstd"], ALU.mult, ALU.mult, [r_xin, r_const, nb["r_rstd"]], [r_xin])
            DMA("pool", outT.rearrange("(k p) t -> p k t", p=128)[:, :, ts_], xin, [r_xin], [], "outT")
        final = ["outT"]
    S.emit(final_waits=final)
    return nc

_O = {"z_ssm": (0, 1024), "xbc": (1024, 2304), "dt": (2304, 2320), "q": (2320, 3344), "k": (3344, 4368),
      "v": (4368, 5392), "z_att": (5392, 6416), "f": (6416, 6424), "g": (6424, 8472)}


def _colmajor(v, nblk):
    return np.ascontiguousarray(np.asarray(v, np.float32).reshape(nblk, 128).T)


def make_inputs(b, x, c, w_ada, b_ada, norm_w, w_in, conv_w, conv_b, dt_bias, a_log, d_skip,
                ssm_norm_w, b_f, b_gate, w_proj_ssm, w_proj_att, w_out, final_norm_w):
    f32 = np.float32
    w_in = np.asarray(w_in, f32)

    def cs(*names):
        return np.ascontiguousarray(np.concatenate([w_in[:, _O[n][0]:_O[n][1]] for n in names], axis=1))

    cols = np.zeros((128, NCOL), f32)
    cols[:, C_C:C_C + 8] = _colmajor(c[b], 8)
    cols[:, C_BADA:C_BADA + 24] = _colmajor(b_ada, 24)
    cols[:, C_NW:C_NW + 8] = _colmajor(norm_w, 8)
    cw = np.asarray(conv_w, f32).reshape(4, 10, 128).transpose(2, 1, 0).reshape(128, 40)
    cols[:, C_CW:C_CW + 40] = cw
    cols[:, C_CB:C_CB + 10] = _colmajor(conv_b, 10)
    cols[:, C_BG:C_BG + 16] = _colmajor(b_gate, 16)
    cols[:, C_FNW:C_FNW + 8] = _colmajor(final_norm_w, 8)
    cols[:64, C_GM] = 1.0
    cols[64:, C_GM + 1] = 1.0
    row = np.concatenate([np.asarray(dt_bias, f32), np.asarray(a_log, f32), np.asarray(b_f, f32),
                          np.repeat(np.asarray(d_skip, f32), 64), np.asarray(ssm_norm_w, f32)])
    rows = np.ascontiguousarray(np.broadcast_to(row[None, :], (128, NROW)))
    j = np.arange(128)
    U = (j[:, None] <= j[None, :]).astype(f32)
    SL = (j[:, None] > j[None, :]).astype(f32)
    half = np.broadcast_to((j[:, None] <= 63), (128, 128)).astype(f32)
    masks = np.ascontiguousarray(np.concatenate([U, SL, half, np.ones((128, 128), f32), np.eye(128, dtype=f32)], axis=1))
    return {
        "xT": np.ascontiguousarray(np.asarray(x[b], f32).T),
        "wA1": cs("q", "k", "v", "f"), "wA2": cs("xbc", "z_ssm", "dt"), "wC": cs("g", "z_att"),
        "wada": np.ascontiguousarray(np.asarray(w_ada, f32)),
        "wps": np.ascontiguousarray(np.asarray(w_proj_ssm, f32)),
        "wpa": np.ascontiguousarray(np.asarray(w_proj_att, f32)),
        "wout": np.ascontiguousarray(np.asarray(w_out, f32)),
        "cols": cols, "rows": rows, "masks": masks,
    }


_NC_CACHE = {}


def kernel(**inputs):
    inputs = {k: np.asarray(v) for k, v in inputs.items()}
    x = inputs["x"]
    B, S_TOK, D = x.shape
    if S_TOK not in _NC_CACHE:
        _NC_CACHE[S_TOK] = build(S_TOK)
    nc = _NC_CACHE[S_TOK]
    in_maps = [make_inputs(i // 2, **inputs) for i in range(8)]
    res = run_bass_kernel_spmd(nc, in_maps, core_ids=list(range(8)))
    out = np.empty((B, S_TOK, D), np.float32)
    half = S_TOK // 2
    for b in range(B):
        out[b, :half] = res.results[2 * b]["outT"][:, :half].T
        out[b, half:] = res.results[2 * b + 1]["outT"][:, half:].T
    return out
```

```python
import numpy as np
import concourse.bass as bass
import concourse.mybir as mybir
from concourse.bass_utils import run_bass_kernel_spmd

F32 = mybir.dt.float32
BF16 = mybir.dt.bfloat16
AF = mybir.ActivationFunctionType
ALU = mybir.AluOpType
AX = mybir.AxisListType


MUTE = [False]


class Res:
    __slots__ = ("w", "r", "name")

    def __init__(self, name=""):
        self.w = None
        self.r = {}
        self.name = name


class Op:
    __slots__ = ("eng", "fn", "deps", "idx", "dma_sem", "dma_cnt", "need_inc", "inc_cnt")


class Sched:
    ENGS = ("pe", "act", "dve", "pool", "sp")

    def __init__(self, nc):
        self.nc = nc
        self.ops = []
        self.per_eng = {e: [] for e in self.ENGS}
        self.dma_sems = {}
        self.bar = {}
        self.last = {}

    def _dep(self, deps, op):
        if op is None:
            return
        if op.dma_sem is not None:
            k = ("d", op.dma_sem)
            if deps.get(k, 0) < op.dma_cnt:
                deps[k] = op.dma_cnt
        else:
            k = ("e", op.eng)
            if k not in deps or deps[k].idx < op.idx:
                deps[k] = op

    def add(self, eng, fn, reads=(), writes=(), dma=None):
        if MUTE[0]:
            return None
        op = Op()
        op.eng = eng
        op.fn = fn
        op.idx = len(self.per_eng[eng])
        op.need_inc = False
        op.inc_cnt = 0
        op.dma_sem = None
        op.dma_cnt = 0
        deps = dict(self.bar)
        for r in reads:
            self._dep(deps, r.w)
        for w in writes:
            self._dep(deps, w.w)
            for ro in w.r.values():
                self._dep(deps, ro)
        if dma is not None:
            ent = self.dma_sems.setdefault(dma, [None, 0])
            ent[1] += 16
            op.dma_sem = dma
            op.dma_cnt = ent[1]
        if eng == "pe":
            deps.pop(("e", "pe"), None)
        op.deps = deps
        for k, d in deps.items():
            if k[0] == "e":
                d.need_inc = True
        rk = ("d", dma) if dma is not None else eng
        for r in reads:
            r.r[rk] = op
        for w in writes:
            w.w = op
            w.r = {}
        self.per_eng[eng].append(op)
        self.ops.append(op)
        if dma is None:
            self.last[eng] = op
        return op

    def barrier(self):
        bar = {}
        for e, op in self.last.items():
            if op.dma_sem is None:
                bar[("e", e)] = op
                op.need_inc = True
        for k, ent in self.dma_sems.items():
            bar[("d", k)] = ent[1]
        self.bar = bar

    def emit(self, final_waits=()):
        nc = self.nc
        sems = {e: nc.alloc_semaphore(name="sem_" + e) for e in self.ENGS}
        for k, ent in self.dma_sems.items():
            ent[0] = nc.alloc_semaphore(name="dsem_%s" % (k,))
        for e in self.ENGS:
            c = 0
            for op in self.per_eng[e]:
                if op.need_inc:
                    c += 1
                    op.inc_cnt = c
        per_eng = self.per_eng
        dma_sems = self.dma_sems

        def run(e, engobj):
            seen = {}
            for op in per_eng[e]:
                for k, d in op.deps.items():
                    if k[0] == "d":
                        sem, val = dma_sems[k[1]][0], d
                    else:
                        sem, val = sems[k[1]], d.inc_cnt
                    if seen.get(k, 0) >= val:
                        continue
                    seen[k] = val
                    engobj.wait_ge(sem, val)
                ins = op.fn(engobj)
                if op.dma_sem is not None:
                    ins.then_inc(dma_sems[op.dma_sem][0], 16)
                elif op.need_inc:
                    ins.then_inc(sems[e], 1)
            if e == "sp":
                for k in final_waits:
                    engobj.wait_ge(dma_sems[k][0], dma_sems[k][1])

        with nc.Block() as block:
            @block.tensor
            def _(eng):
                run("pe", eng)

            @block.scalar
            def _(eng):
                run("act", eng)

            @block.vector
            def _(eng):
                run("dve", eng)

            @block.gpsimd
            def _(eng):
                run("pool", eng)

            @block.sync
            def _(eng):
                run("sp", eng)


class Arena:
    def __init__(self, nc, words=51200):
        self.nc = nc
        self.big = nc.alloc_sbuf_tensor("arena", [128, words], F32)
        self.off = 0
        self.limit = words

    def mark(self):
        return self.off

    def reset(self, m):
        self.off = m

    def alloc(self, shape, dtype, name=None):
        n = int(np.prod(shape[1:]))
        words = n if dtype == F32 else (n + 1) // 2
        words = (words + 15) // 16 * 16
        assert self.off + words <= self.limit, ("SBUF overflow", name, self.off, words)
        v = self.big[:, self.off:self.off + words]
        self.off += words
        if dtype != F32:
            v = v.bitcast(dtype)
        v = v[:, 0:n]
        if len(shape) == 3:
            v = v.rearrange("p (a b) -> p a b", b=shape[2])
        elif len(shape) == 4:
            v = v.rearrange("p (a b c) -> p a b c", b=shape[2], c=shape[3])
        return v


class Psum:
    def __init__(self, nc):
        self.t = nc.alloc_psum_tensor("psum_all", [128, 4096], F32)

    def bank(self, b, n=1):
        return self.t[:, 512 * b:512 * (b + n)]

EPS = 1e-6
SCALE = 128 ** -0.5

C_C, C_BADA, C_NW, C_CW, C_CB, C_BG, C_FNW, C_GM, C_VAL, C_PEN, NCOL = 0, 8, 32, 40, 80, 90, 106, 114, 116, 117, 118
R_DTB, R_ALOG, R_BF, R_D, R_SW, NROW = 0, 16, 32, 40, 1064, 2088


DBG = [0]


def dbg(n):
    if DBG[0] == n:
        MUTE[0] = True


def build(S_TOK, phases="12BC"):
    NT = S_TOK // 512
    NB = S_TOK // 128
    HALF = NT // 2
    NBH = NB // 2
    BQ0 = HALF
    nc = bass.Bass("TRN2", target_bir_lowering=False)

    def din(name, shape, dt=F32):
        return nc.dram_tensor(name, shape, dt, kind="ExternalInput").ap()

    xT = din("xT", [1024, S_TOK])
    wA1 = din("wA1", [1024, 3080])
    wA2 = din("wA2", [1024, 2320])
    wC = din("wC", [1024, 3072])
    wada = din("wada", [1024, 3072])
    wps = din("wps", [1024, 1024])
    wpa = din("wpa", [1024, 1024])
    wout = din("wout", [1024, 1024])
    cols_d = din("cols", [128, NCOL])
    rows_d = din("rows", [128, NROW])
    masks_d = din("masks", [128, 640])
    outT = nc.dram_tensor("outT", [1024, S_TOK // 2], F32, kind="ExternalOutput").ap()
    KT = nc.dram_tensor("KT", [8, 128, S_TOK], BF16, kind="Internal").ap()
    QT = nc.dram_tensor("QT", [8, 128, S_TOK], BF16, kind="Internal").ap()
    VS = nc.dram_tensor("VS", [S_TOK, 1024], BF16, kind="Internal").ap()
    YA = nc.dram_tensor("YA", [1024, S_TOK], BF16, kind="Internal").ap()
    YW = nc.dram_tensor("YW", [1024, S_TOK], BF16, kind="Internal").ap()

    S = Sched(nc)
    A = Arena(nc)
    PS = Psum(nc)
    pbank = [(PS.bank(b), Res("pb%d" % b)) for b in range(8)]
    pb_rr = [0]

    def nextbank():
        b = pbank[pb_rr[0] % 8]
        pb_rr[0] += 1
        return b

    def MM(out, lhsT, rhs, st, sp, R, W):
        S.add("pe", lambda e: e.matmul(out, lhsT=lhsT, rhs=rhs, start=st, stop=sp), R, W)

    def TR(out, in_, ident, R, W):
        S.add("pe", lambda e: e.transpose(out, in_, ident), R, W)

    def ACT(out, in_, func, R, W, bias=None, scale=None, accum=None):
        kw = {}
        if bias is not None:
            kw["bias"] = bias
        if scale is not None:
            kw["scale"] = scale
        if accum is not None:
            kw["accum_out"] = accum
        S.add("act", lambda e: e.activation(out=out, in_=in_, func=func, **kw), R, W)

    def TT(eng, out, in0, in1, op, R, W):
        S.add(eng, lambda e: e.tensor_tensor(out=out, in0=in0, in1=in1, op=op), R, W)

    def TS(eng, out, in0, s1, s2, op0, op1, R, W):
        if op1 is None:
            S.add(eng, lambda e: e.tensor_scalar(out=out, in0=in0, scalar1=s1, scalar2=None, op0=op0), R, W)
        else:
            S.add(eng, lambda e: e.tensor_scalar(out=out, in0=in0, scalar1=s1, scalar2=s2, op0=op0, op1=op1), R, W)

    def STT(eng, out, in0, scalar, in1, op0, op1, R, W):
        S.add(eng, lambda e: e.scalar_tensor_tensor(out=out, in0=in0, scalar=scalar, in1=in1, op0=op0, op1=op1), R, W)

    def CP(eng, out, in_, R, W):
        if eng == "act":
            ACT(out, in_, AF.Copy, R, W)
        else:
            S.add(eng, lambda e: e.tensor_copy(out=out, in_=in_), R, W)

    def MEMSET(eng, ap, val, W):
        S.add(eng, lambda e: e.memset(ap, val), [], W)

    def DMA(q, out, in_, R, W, key):
        S.add(q, lambda e: e.dma_start(out=out, in_=in_), R, W, dma=key)

    cols = A.alloc([128, NCOL], F32, "cols")
    rows = A.alloc([128, NROW], F32, "rows")
    mk = A.alloc([128, 640], F32, "masks")
    r_const = Res("const")
    DMA("sp", cols, cols_d, [], [r_const], "c0")
    DMA("sp", rows, rows_d, [], [r_const], "c1")
    DMA("sp", mk, masks_d, [], [r_const], "c2")
    U_f, SL_f, half_f, ones_f, id_f = (mk[:, 128 * i:128 * (i + 1)] for i in range(5))
    mkb = A.alloc([128, 384], BF16, "masksb")
    r_cb = Res("constb")
    CP("dve", mkb[:, 0:128], U_f, [r_const], [r_cb])
    CP("dve", mkb[:, 128:256], ones_f, [r_const], [r_cb])
    CP("dve", mkb[:, 256:384], id_f, [r_const], [r_cb])
    U_b, ones_b, id_b = mkb[:, 0:128], mkb[:, 128:256], mkb[:, 256:384]
    adac = A.alloc([128, 32], F32, "adac")
    r_ada = Res("ada")
    aneg = A.alloc([128, 16], F32, "aneg")
    Ftok = A.alloc([128, NB, 8], F32, "Ftok")
    Cend = A.alloc([128, NB, 8], F32, "Cend")
    Cmid = A.alloc([128, NB, 8], F32, "Cmid")
    nCend = A.alloc([128, NB, 8], F32, "nCend")
    zero8 = A.alloc([128, 8], F32, "zero8")
    r_F = Res("F")
    MEMSET("pool", zero8, 0.0, [r_F])
    stage = [(A.alloc([128, 1024], F32, "stage%d" % i), Res("stage%d" % i)) for i in range(2)]
    st_rr = [0]
    base_mark = A.mark()

    def load_w(dst, r_dst, src, ncols):
        for k in range(8):
            c0 = 0
            while c0 < ncols:
                n = min(1024, ncols - c0)
                stg, r_s = stage[st_rr[0] % 2]
                key = "stage%d" % (st_rr[0] % 2)
                st_rr[0] += 1
                DMA("sp", stg[:, 0:n], src[k * 128:(k + 1) * 128, c0:c0 + n], [], [r_s], key)
                CP("pool" if (st_rr[0] % 2) else "dve", dst[:, k, c0:c0 + n], stg[:, 0:n], [r_s], [r_dst])
                c0 += n

    m0 = A.mark()
    wada_sb = A.alloc([128, 8, 3072], F32, "wada")
    r_wada = Res("wada")
    for k in range(8):
        DMA("sp", wada_sb[:, k, :], wada[k * 128:(k + 1) * 128, :], [], [r_wada], "wada")
    sc = A.alloc([128, 8], F32, "silu_c")
    r_sc = Res("sc")
    ACT(sc, cols[:, C_C:C_C + 8], AF.Silu, [r_const], [r_sc])
    pb, r_pb = nextbank()
    for eb in range(24):
        for k in range(8):
            MM(pb[:, eb:eb + 1], wada_sb[:, k, eb * 128:(eb + 1) * 128], sc[:, k:k + 1], k == 0, k == 7,
               [r_wada, r_sc], [r_pb])
    adaf = A.alloc([128, 24], F32, "adaf")
    TT("dve", adaf, pb[:, 0:24], cols[:, C_BADA:C_BADA + 24], ALU.add, [r_pb, r_const], [r_ada])
    TS("dve", adac[:, 24:32], adaf[:, 8:16], 1.0, None, ALU.add, None, [r_ada], [r_ada])
    TT("dve", adac[:, 0:8], adac[:, 24:32], cols[:, C_NW:C_NW + 8], ALU.mult, [r_ada, r_const], [r_ada])
    CP("dve", adac[:, 8:16], adaf[:, 0:8], [r_ada], [r_ada])
    CP("dve", adac[:, 16:24], adaf[:, 16:24], [r_ada], [r_ada])
    ACT(aneg, rows[:, R_ALOG:R_ALOG + 16], AF.Exp, [r_const], [r_ada])
    TS("dve", aneg, aneg, -1.0, None, ALU.mult, None, [r_ada], [r_ada])
    S.barrier()
    A.reset(m0)

    def norm_bufs():
        d = {}
        d["xin"] = A.alloc([128, 8, 512], F32, "xin"); d["r_xin"] = Res("xin")
        d["sq"] = A.alloc([128, 8, 512], BF16, "sq"); d["r_sq"] = Res("sq")
        d["hT"] = A.alloc([128, 8, 512], BF16, "hT"); d["r_hT"] = Res("hT")
        d["tmp"] = [(A.alloc([128, 512], F32, "tmp"), Res("tmp")) for _ in range(2)]
        d["rstd"] = A.alloc([128, 512], F32, "rstd"); d["r_rstd"] = Res("rstd")
        return d

    def rms_rstd(nb, src, r_src):
        for k in range(8):
            ACT(nb["sq"][:, k, :], src[:, k, :], AF.Square, [r_src], [nb["r_sq"]])
        pb, r_pb = nextbank()
        for k in range(8):
            MM(pb, ones_b, nb["sq"][:, k, :], k == 0, k == 7, [nb["r_sq"], r_cb], [r_pb])
        TS("dve", nb["rstd"], pb, 1.0 / 1024, EPS, ALU.mult, ALU.add, [r_pb], [nb["r_rstd"]])
        ACT(nb["rstd"], nb["rstd"], AF.Ln, [nb["r_rstd"]], [nb["r_rstd"]])
        ACT(nb["rstd"], nb["rstd"], AF.Exp, [nb["r_rstd"]], [nb["r_rstd"]], scale=-0.5)

    def load_x(nb, i):
        DMA("sp", nb["xin"], xT.rearrange("(k p) t -> p k t", p=128)[:, :, i * 512:(i + 1) * 512],
            [], [nb["r_xin"]], "xin")

    def norm_tile(nb):
        rms_rstd(nb, nb["xin"], nb["r_xin"])
        for k in range(8):
            tmp, r_tmp = nb["tmp"][k % 2]
            TT("dve", tmp, nb["xin"][:, k, :], nb["rstd"], ALU.mult, [nb["r_xin"], nb["r_rstd"]], [r_tmp])
            ACT(nb["hT"][:, k, :], tmp, AF.Identity, [r_tmp, r_ada], [nb["r_hT"]],
                bias=adac[:, 8 + k:9 + k], scale=adac[:, k:k + 1])

    ev_rr = [0]

    def evac(out, in_, R, W):
        e = "act" if ev_rr[0] % 2 == 0 else "dve"
        ev_rr[0] += 1
        CP(e, out, in_, R, W)

    if "1" in phases:
        m1 = A.mark()
        w1 = A.alloc([128, 8, 3080], BF16, "w1"); r_w1 = Res("w1")
        load_w(w1, r_w1, wA1, 3080)
        nb = norm_bufs()
        kq = A.alloc([128, 16, 512], BF16, "kq"); r_kq = Res("kq")
        vt = [(A.alloc([128, 1024], BF16, "vt"), Res("vt")) for _ in range(2)]
        fsm = A.alloc([128, 32], F32, "fsm"); r_fsm = Res("fsm")
        for i in range(NT):
            load_x(nb, i)
            norm_tile(nb)
            hT, r_hT = nb["hT"], nb["r_hT"]
            for blk in (range(16) if i >= HALF else range(8, 16)):
                pb, r_pb = nextbank()
                for k in range(8):
                    MM(pb, w1[:, k, blk * 128:(blk + 1) * 128], hT[:, k, :], k == 0, k == 7, [r_w1, r_hT], [r_pb])
                evac(kq[:, blk, :], pb, [r_pb], [r_kq])
            if i >= HALF:
                DMA("pool", QT.rearrange("h d t -> d h t")[:, :, i * 512:(i + 1) * 512], kq[:, 0:8, :], [r_kq], [], "kq")
            DMA("pool", KT.rearrange("h d t -> d h t")[:, :, i * 512:(i + 1) * 512], kq[:, 8:16, :], [r_kq], [], "kq")
            for c in range(4):
                blk = i * 4 + c
                tsl = slice(c * 128, (c + 1) * 128)
                v_sb, r_v = vt[c % 2]
                for hf in range(2):
                    pb, r_pb = nextbank()
                    for k in range(8):
                        MM(pb, hT[:, k, tsl], w1[:, k, 2048 + hf * 512:2048 + (hf + 1) * 512], k == 0, k == 7,
                           [r_w1, r_hT], [r_pb])
                    evac(v_sb[:, hf * 512:(hf + 1) * 512], pb, [r_pb], [r_v])
                DMA("pool", VS[blk * 128:(blk + 1) * 128, :], v_sb, [r_v], [], "vt%d" % (c % 2))
                pb, r_pb = nextbank()
                for k in range(8):
                    MM(pb[:, 0:8], hT[:, k, tsl], w1[:, k, 3072:3080], k == 0, k == 7, [r_w1, r_hT], [r_pb])
                TT("dve", fsm[:, 0:8], pb[:, 0:8], rows[:, R_BF:R_BF + 8], ALU.add, [r_pb, r_const], [r_fsm])
                ACT(fsm[:, 8:16], fsm[:, 0:8], AF.Exp, [r_fsm], [r_fsm], scale=-1.0)
                ACT(fsm[:, 16:24], fsm[:, 8:16], AF.Ln, [r_fsm], [r_fsm], bias=1.0)
                TS("dve", fsm[:, 24:32], fsm[:, 16:24], -1.0, None, ALU.mult, None, [r_fsm], [r_fsm])
                lf = fsm[:, 24:32]
                pb, r_pb = nextbank()
                MM(pb[:, 0:8], U_f, lf, True, True, [r_fsm, r_const], [r_pb])
                MM(pb[:, 8:16], ones_f, lf, True, True, [r_fsm, r_const], [r_pb])
                MM(pb[:, 16:24], half_f, lf, True, True, [r_fsm, r_const], [r_pb])
                carry = zero8 if blk == 0 else Cend[:, blk - 1, :]
                TT("dve", Ftok[:, blk, :], pb[:, 0:8], carry, ALU.add, [r_pb, r_F], [r_F])
                TT("dve", Cmid[:, blk, :], pb[:, 16:24], carry, ALU.add, [r_pb, r_F], [r_F])
                TT("dve", Cend[:, blk, :], pb[:, 8:16], carry, ALU.add, [r_pb, r_F], [r_F])
                if blk == NBH - 1:
                    TS("dve", Cend[:, blk, :], Cend[:, blk, :], cols[:, C_VAL:C_VAL + 1], None, ALU.mult, None,
                       [r_F, r_const], [r_F])
                    TS("dve", Ftok[:, 0:NBH, :], Ftok[:, 0:NBH, :], cols[:, C_VAL:C_VAL + 1],
                       cols[:, C_PEN:C_PEN + 1], ALU.mult, ALU.add, [r_F, r_const], [r_F])
                TS("dve", nCend[:, blk, :], Cend[:, blk, :], -1.0, None, ALU.mult, None, [r_F], [r_F])
        S.barrier()
        A.reset(m1)

    if "2" in phases:
        m1 = A.mark()
        w2 = A.alloc([128, 8, 2320], BF16, "w2"); r_w2 = Res("w2")
        load_w(w2, r_w2, wA2, 2320)
        nb = norm_bufs()
        hT, r_hT = nb["hT"], nb["r_hT"]
        ubuf = A.alloc([128, 10, 515], F32, "ubuf"); r_u = Res("ubuf")
        MEMSET("pool", ubuf[:, :, 0:3], 0.0, [r_u])
        cacc = [(A.alloc([128, 512], F32, "cacc"), Res("cacc")) for _ in range(2)]
        xsT = A.alloc([128, 10, 512], BF16, "xsT"); r_xsT = Res("xsT")
        bcg = A.alloc([128, 4, 512], BF16, "bcg"); r_bcg = Res("bcg")
        xs_tok = A.alloc([128, 1024], BF16, "xs_tok"); r_xst = Res("xs_tok")
        B_tok = A.alloc([128, 128], BF16, "B_tok"); r_Bt = Res("B_tok")
        sm = A.alloc([128, 160], F32, "sm"); r_sm = Res("sm")
        zs = A.alloc([128, 1024], BF16, "zs"); r_zs = Res("zs")
        Rb = A.alloc([128, 16, 128], F32, "Rb"); r_R = Res("R")
        cbm = A.alloc([128, 2, 128], F32, "cbm"); r_cbm = Res("cbm")
        eseg = [(A.alloc([128, 512], F32, "eseg"), Res("eseg")) for _ in range(2)]
        MT = A.alloc([128, 16, 128], BF16, "MT"); r_MT = Res("MT")
        xdt = A.alloc([128, 1024], BF16, "xdt"); r_xdt = Res("xdt")
        xdtd = A.alloc([128, 1024], BF16, "xdtd"); r_xdtd = Res("xdtd")
        Sst = A.alloc([128, 512], F32, "Sst"); r_S = Res("Sst")
        Sbf = A.alloc([128, 512], BF16, "Sbf"); r_Sb = Res("Sbf")
        y1 = A.alloc([128, 1024], F32, "y1"); r_y1 = Res("y1")
        y2 = A.alloc([128, 1024], F32, "y2"); r_y2 = Res("y2")
        junk = A.alloc([128, 1024], BF16, "junk"); r_junk = Res("junk")
        Yn = A.alloc([128, 1024], BF16, "Yn"); r_Yn = Res("Yn")
        YwT = A.alloc([128, 8, 512], BF16, "YwT"); r_YwT = Res("YwT")
        MEMSET("pool", Sst, 0.0, [r_S])
        MEMSET("pool", Sbf, 0.0, [r_Sb])
        dtraw, dtx, dt, a_t, acs, atot, eacs, dd, dec, eatot, ssq = (
            sm[:, 0:16], sm[:, 16:32], sm[:, 32:48], sm[:, 48:64], sm[:, 64:80], sm[:, 80:96],
            sm[:, 96:112], sm[:, 112:128], sm[:, 128:144], sm[:, 144:160], None)
        sm2 = A.alloc([128, 16], F32, "sm2"); r_sm2 = Res("sm2")
        for i in range(NT):
            load_x(nb, i)
            norm_tile(nb)
            pass
            own = i >= HALF
            if i == HALF:
                TS("pool", ubuf[:, :, 0:3], ubuf[:, :, 512:515], cols[:, C_VAL:C_VAL + 1], None, ALU.mult, None,
                   [r_u, r_const], [r_u])
                TS("dve", Sst, Sst, cols[:, C_VAL:C_VAL + 1], None, ALU.mult, None, [r_S, r_const], [r_S])
                CP("pool", Sbf, Sst, [r_S], [r_Sb])
            elif i > 0:
                CP("pool", ubuf[:, :, 0:3], ubuf[:, :, 512:515], [r_u], [r_u])
            for blk in range(10):
                pb, r_pb = nextbank()
                for k in range(8):
                    MM(pb, w2[:, k, blk * 128:(blk + 1) * 128], hT[:, k, :], k == 0, k == 7, [r_w2, r_hT], [r_pb])
                evac(ubuf[:, blk, 3:515], pb, [r_pb], [r_u])
            pass
            for blk in range(10):
                ca, r_ca = cacc[blk % 2]
                cw = C_CW + blk * 4
                TS("dve", ca, ubuf[:, blk, 0:512], cols[:, cw:cw + 1], cols[:, C_CB + blk:C_CB + blk + 1],
                   ALU.mult, ALU.add, [r_u, r_const], [r_ca])
                for tap in range(1, 4):
                    STT("dve", ca, ubuf[:, blk, tap:tap + 512], cols[:, cw + tap:cw + tap + 1], ca,
                        ALU.mult, ALU.add, [r_u, r_const, r_ca], [r_ca])
                ACT(xsT[:, blk, :], ca, AF.Silu, [r_ca], [r_xsT])
            for g in range(2):
                TS("pool", bcg[:, g, :], xsT[:, 8, :], cols[:, C_GM + g:C_GM + g + 1], None, ALU.mult, None,
                   [r_xsT, r_const], [r_bcg])
                TS("pool", bcg[:, 2 + g, :], xsT[:, 9, :], cols[:, C_GM + g:C_GM + g + 1], None, ALU.mult, None,
                   [r_xsT, r_const], [r_bcg])
            pass
            for c in range(4):
                tsl = slice(c * 128, (c + 1) * 128)
                MUTE[0] = not own
                for hf in range(2):
                    pb, r_pb = nextbank()
                    for k in range(8):
                        MM(pb, hT[:, k, tsl], w2[:, k, 1280 + hf * 512:1280 + (hf + 1) * 512], k == 0, k == 7,
                           [r_w2, r_hT], [r_pb])
                    ACT(zs[:, hf * 512:(hf + 1) * 512], pb, AF.Silu, [r_pb], [r_zs])
                MUTE[0] = False
                pb, r_pb = nextbank()
                for k in range(8):
                    MM(pb[:, 0:16], hT[:, k, tsl], w2[:, k, 2304:2320], k == 0, k == 7, [r_w2, r_hT], [r_pb])
                TT("dve", dtraw, pb[:, 0:16], rows[:, R_DTB:R_DTB + 16], ALU.add, [r_pb, r_const], [r_sm])
                ACT(dtx, dtraw, AF.Exp, [r_sm], [r_sm])
                ACT(dt, dtx, AF.Ln, [r_sm], [r_sm], bias=1.0)
                TT("dve", a_t, dt, aneg, ALU.mult, [r_sm, r_ada], [r_sm])
                pass
                pbt, r_pbt = nextbank()
                pbt_b = pbt.bitcast(BF16)
                for blk in range(8):
                    TR(pbt_b[:, blk * 128:(blk + 1) * 128], xsT[:, blk, tsl], id_b, [r_xsT, r_cb], [r_pbt])
                evac(xs_tok, pbt_b, [r_pbt], [r_xst])
                pbt, r_pbt = nextbank()
                pbt_b = pbt.bitcast(BF16)
                TR(pbt_b[:, 0:128], xsT[:, 8, tsl], id_b, [r_xsT, r_cb], [r_pbt])
                evac(B_tok, pbt_b[:, 0:128], [r_pbt], [r_Bt])
                pass
                pb, r_pb = nextbank()
                MM(pb[:, 0:16], U_f, a_t, True, True, [r_sm, r_const], [r_pb])
                MM(pb[:, 16:32], ones_f, a_t, True, True, [r_sm, r_const], [r_pb])
                CP("dve", sm[:, 64:96], pb[:, 0:32], [r_pb], [r_sm])
                ACT(eacs, acs, AF.Exp, [r_sm], [r_sm])
                TT("dve", dd, atot, acs, ALU.subtract, [r_sm], [r_sm])
                ACT(dec, dd, AF.Exp, [r_sm], [r_sm])
                ACT(eatot, atot, AF.Exp, [r_sm], [r_sm])
                MUTE[0] = not own
                TT("dve", Rb, U_f.unsqueeze(1).to_broadcast([128, 16, 128]),
                   a_t.unsqueeze(2).to_broadcast([128, 16, 128]), ALU.mult, [r_sm, r_const], [r_R])
                pass
                pb, r_pb = nextbank()
                for g in range(2):
                    MM(pb[:, g * 128:(g + 1) * 128], bcg[:, g, tsl], xsT[:, 9, tsl],
                       True, True, [r_xsT, r_bcg], [r_pb])
                pass
                TT("dve", cbm, pb[:, 0:256].rearrange("p (g l) -> p g l", g=2),
                   U_f.unsqueeze(1).to_broadcast([128, 2, 128]), ALU.mult, [r_pb, r_const], [r_cbm])
                pass
                for q4 in range(4):
                    pb, r_pb = nextbank()
                    MM(pb, SL_f, Rb[:, 4 * q4:4 * q4 + 4, :].rearrange("p h l -> p (h l)"), True, True,
                       [r_R, r_const], [r_pb])
                    es, r_es = eseg[q4 % 2]
                    ACT(es, pb, AF.Exp, [r_pb], [r_es])
                    g = q4 // 2
                    TT("dve", MT[:, 4 * q4:4 * q4 + 4, :], es.rearrange("p (h l) -> p h l", h=4),
                       cbm[:, g:g + 1, :].to_broadcast([128, 4, 128]), ALU.mult, [r_es, r_cbm], [r_MT])
                MUTE[0] = False
                TT("dve", xdt.rearrange("p (h d) -> p h d", h=16), xs_tok.rearrange("p (h d) -> p h d", h=16),
                   dt.unsqueeze(2).to_broadcast([128, 16, 64]), ALU.mult, [r_xst, r_sm], [r_xdt])
                TT("pool", xdtd.rearrange("p (h d) -> p h d", h=16), xdt.rearrange("p (h d) -> p h d", h=16),
                   dec.unsqueeze(2).to_broadcast([128, 16, 64]), ALU.mult, [r_xdt, r_sm], [r_xdtd])
                MUTE[0] = not own
                yd = []
                for hf in range(2):
                    pb, r_pb = nextbank()
                    for hh in range(8):
                        h = hf * 8 + hh
                        MM(pb[:, hh * 64:(hh + 1) * 64], MT[:, h, :], xdt[:, h * 64:(h + 1) * 64], True, True,
                           [r_MT, r_xdt], [r_pb])
                    yd.append((pb, r_pb))
                yo = []
                for g in range(2):
                    pb, r_pb = nextbank()
                    MM(pb, bcg[:, 2 + g, tsl], Sbf, True, True, [r_bcg, r_Sb], [r_pb])
                    yo.append((pb, r_pb))
                pass
                for g in range(2):
                    sl = slice(g * 512, (g + 1) * 512)
                    TT("dve", y1[:, sl].rearrange("p (h d) -> p h d", h=8),
                       yo[g][0].rearrange("p (h d) -> p h d", h=8),
                       eacs[:, 8 * g:8 * g + 8].unsqueeze(2).to_broadcast([128, 8, 64]), ALU.mult,
                       [yo[g][1], r_sm], [r_y1])
                    TT("dve", y1[:, sl], yd[g][0], y1[:, sl], ALU.add, [yd[g][1], r_y1], [r_y1])
                TT("pool", y2, xs_tok, rows[:, R_D:R_D + 1024], ALU.mult, [r_xst, r_const], [r_y2])
                TT("pool", y1, y1, y2, ALU.add, [r_y1, r_y2], [r_y1])
                TT("dve", y2, y1, zs, ALU.mult, [r_y1, r_zs], [r_y2])
                ACT(junk, y2, AF.Square, [r_y2], [r_junk, r_sm2], accum=sm2[:, 0:1])
                TS("dve", sm2[:, 1:2], sm2[:, 0:1], 1.0 / 1024, EPS, ALU.mult, ALU.add, [r_sm2], [r_sm2])
                ACT(sm2[:, 2:3], sm2[:, 1:2], AF.Ln, [r_sm2], [r_sm2])
                ACT(sm2[:, 3:4], sm2[:, 2:3], AF.Exp, [r_sm2], [r_sm2], scale=-0.5)
                STT("dve", Yn, y2, sm2[:, 3:4], rows[:, R_SW:R_SW + 1024], ALU.mult, ALU.mult,
                    [r_y2, r_sm2, r_const], [r_Yn])
                pass
                pbt, r_pbt = nextbank()
                pbt_b = pbt.bitcast(BF16)
                for blk in range(8):
                    TR(pbt_b[:, blk * 128:(blk + 1) * 128], Yn[:, blk * 128:(blk + 1) * 128], id_b, [r_Yn, r_cb], [r_pbt])
                evac(YwT[:, :, tsl], pbt_b.rearrange("p (b t) -> p b t", b=8), [r_pbt], [r_YwT])
                MUTE[0] = False
                for g in range(2):
                    pb, r_pb = nextbank()
                    MM(pb, B_tok, xdtd[:, g * 512:(g + 1) * 512], True, True, [r_Bt, r_xdtd], [r_pb])
                    ps_ = slice(64 * g, 64 * g + 64)
                    TT("dve", Sst[ps_, :].rearrange("p (h d) -> p h d", h=8),
                       Sst[ps_, :].rearrange("p (h d) -> p h d", h=8),
                       eatot[ps_, 8 * g:8 * g + 8].unsqueeze(2).to_broadcast([64, 8, 64]), ALU.mult,
                       [r_S, r_sm], [r_S])
                    TT("dve", Sst[ps_, :], pb[ps_, :], Sst[ps_, :], ALU.add, [r_pb, r_S], [r_S])
                    CP("pool", Sbf[ps_, :], Sst[ps_, :], [r_S], [r_Sb])
            if own:
                DMA("pool", YW.rearrange("(k p) t -> p k t", p=128)[:, :, i * 512:(i + 1) * 512], YwT, [r_YwT], [], "YwT")
        S.barrier()
        A.reset(m1)

    if "B" in phases:
        m1 = A.mark()
        hb = []
        for s_ in range(2):
            hb.append(dict(K=A.alloc([128, S_TOK], BF16, "Kh"), V=A.alloc([128, NB, 128], BF16, "Vh"),
                           Q=A.alloc([128, S_TOK], BF16, "Qh"), r=Res("hb%d" % s_)))
        bfar = [(A.alloc([128, NB], F32, "bfar"), Res("bfar")) for _ in range(2)]
        bnear = [(A.alloc([128, 4, 4], F32, "bnear"), Res("bnear")) for _ in range(2)]
        fac = [(A.alloc([128, 4], F32, "fac"), Res("fac")) for _ in range(2)]
        pTf = [(A.alloc([128, 512], BF16, "pTf"), Res("pTf")) for _ in range(4)]
        pTn = [[(A.alloc([128, 512], BF16, "pTn"), Res("pTn")) for _ in range(4)] for _ in range(2)]
        for st_ in range(2):
            for kb in range(1, 4):
                MEMSET("pool", pTn[st_][kb][0][:, 0:128 * kb], 0.0, [pTn[st_][kb][1]])
        onb = [(A.alloc([128, 512], F32, "on_sb"), Res("on")) for _ in range(2)]
        dnb = [(A.alloc([128, 512], F32, "dn_sb"), Res("dn")) for _ in range(2)]
        yat = [(A.alloc([128, 512], BF16, "yat"), Res("yat")) for _ in range(2)]
        st_b = [pbank[0], pbank[1], pbank[2]]
        ofar, r_ofar = pbank[3]
        dfar, r_dfar = pbank[4]
        onear, r_onear = pbank[5]
        dnear, r_dnear = pbank[6]

        def load_head(h):
            d = hb[h % 2]
            key = "hb%d" % (h % 2)
            DMA("sp", d["K"], KT[h], [], [d["r"]], key)
            DMA("sp", d["Q"][:, HALF * 512:], QT[h][:, HALF * 512:], [], [d["r"]], key)
            vsrc = VS.rearrange("(b p) (h d) -> p b h d", p=128, h=8)
            nstep = max(1, NB // 8)
            for b0 in range(0, NB, nstep):
                DMA("sp", d["V"][:, b0:b0 + nstep, :], vsrc[:, b0:b0 + nstep, h, :], [], [d["r"]], key)

        jobs = []
        tix = 0
        for h in range(8):
            for i in range(BQ0, NT):
                nfar = 4 * i
                for c in range(nfar):
                    jobs.append(dict(h=h, i=i, far=True, c=c, kb=0, first=(c == 0), last=False, tix=tix, nfar=nfar))
                for kb in range(4):
                    jobs.append(dict(h=h, i=i, far=False, c=nfar + kb, kb=kb, first=(nfar == 0 and kb == 0),
                                     last=(kb == 3), tix=tix, nfar=nfar))
                tix += 1

        def prep(j):
            h, i, nfar, p = j["h"], j["i"], j["nfar"], j["tix"] % 2
            if i == BQ0 and h == 0:
                load_head(0)
                load_head(1)
            if nfar > 0:
                TS("dve", bfar[p][0][:, 0:nfar], Ftok[:, 0:nfar, h], -1.0, Cend[:, nfar - 1, h:h + 1],
                   ALU.mult, ALU.add, [r_F], [bfar[p][1]])
                ACT(fac[p][0], Cmid[:, nfar:nfar + 4, h], AF.Exp, [r_F], [fac[p][1]],
                    bias=nCend[:, nfar - 1, h:h + 1])
            TT("dve", bnear[p][0], Cmid[:, nfar:nfar + 4, h].unsqueeze(1).to_broadcast([128, 4, 4]),
               Ftok[:, nfar:nfar + 4, h].unsqueeze(2).to_broadcast([128, 4, 4]), ALU.subtract,
               [r_F], [bnear[p][1]])

        def rec_qk(n):
            j = jobs[n]
            if j["first"]:
                prep(j)
            d = hb[j["h"] % 2]
            stp, r_st = st_b[n % 3]
            c = j["c"]
            qs = slice(j["i"] * 512, j["i"] * 512 + 512)
            MM(stp, d["K"][:, c * 128:(c + 1) * 128], d["Q"][:, qs], True, True, [d["r"]], [r_st])

        pfc = [0]

        def rec_rest(n):
            j = jobs[n]
            h, i, nfar, p, c = j["h"], j["i"], j["nfar"], j["tix"] % 2, j["c"]
            d = hb[h % 2]
            Vh, r_h = d["V"], d["r"]
            stp, r_st = st_b[n % 3]
            if j["far"]:
                pT, r_pT = pTf[pfc[0] % 4]
                pfc[0] += 1
                ACT(pT, stp, AF.Exp, [r_st, bfar[p][1]], [r_pT], bias=bfar[p][0][:, c:c + 1], scale=SCALE)
                MM(ofar, Vh[:, c, :], pT, c == 0, c == nfar - 1, [r_h, r_pT], [r_ofar])
                MM(dfar, ones_b, pT, c == 0, c == nfar - 1, [r_cb, r_pT], [r_dfar])
                return
            kb = j["kb"]
            pT, r_pT = pTn[p][kb]
            for jj in range(kb, 4):
                js = slice(128 * jj, 128 * (jj + 1))
                ACT(pT[:, js], stp[:, js], AF.Exp, [r_st, bnear[p][1]], [r_pT], bias=bnear[p][0][:, kb, jj:jj + 1],
                    scale=SCALE)
            ks = slice(128 * kb, 128 * (kb + 1))
            TT("pool", pT[:, ks], pT[:, ks], U_b, ALU.mult, [r_pT, r_cb], [r_pT])
            MM(onear, Vh[:, c, :], pT, kb == 0, kb == 3, [r_h, r_pT], [r_onear])
            MM(dnear, ones_b, pT, kb == 0, kb == 3, [r_cb, r_pT], [r_dnear])
            if not j["last"]:
                return
            on_sb, r_on = onb[p]
            dn_sb, r_dn = dnb[p]
            CP("dve", on_sb, onear, [r_onear], [r_on])
            CP("dve", dn_sb, dnear, [r_dnear], [r_dn])
            if nfar > 0:
                for jj in range(4):
                    js = slice(128 * jj, 128 * (jj + 1))
                    STT("dve", on_sb[:, js], ofar[:, js], fac[p][0][:, jj:jj + 1], on_sb[:, js], ALU.mult, ALU.add,
                        [r_ofar, fac[p][1], r_on], [r_on])
                    STT("dve", dn_sb[:, js], dfar[:, js], fac[p][0][:, jj:jj + 1], dn_sb[:, js], ALU.mult, ALU.add,
                        [r_dfar, fac[p][1], r_dn], [r_dn])
            S.add("dve", lambda e, o=dn_sb: e.reciprocal(out=o, in_=o), [r_dn], [r_dn])
            ya, r_ya = yat[p]
            TT("dve", ya, on_sb, dn_sb, ALU.mult, [r_on, r_dn], [r_ya])
            qs = slice(i * 512, i * 512 + 512)
            DMA("pool", YA[h * 128:(h + 1) * 128, qs], ya, [r_ya], [], "yat%d" % p)
            if i == NT - 1 and h + 2 < 8:
                load_head(h + 2)

        LOOK = 2
        for n in range(len(jobs) + LOOK):
            if n < len(jobs):
                rec_qk(n)
            if n - LOOK >= 0:
                rec_rest(n - LOOK)
        S.barrier()
        A.reset(m1)

    MUTE[0] = False
    final = []
    if "C" in phases:
        w3 = A.alloc([128, 8, 3072], BF16, "w3"); r_w3 = Res("w3")
        load_w(w3, r_w3, wC, 3072)
        wp = A.alloc([128, 3, 8, 1024], BF16, "wp"); r_wp = Res("wp")
        for j, src in enumerate((wps, wpa, wout)):
            load_w(wp[:, j], r_wp, src, 1024)
        nb = norm_bufs()
        hT, r_hT = nb["hT"], nb["r_hT"]
        xin, r_xin = nb["xin"], nb["r_xin"]
        YAt = A.alloc([128, 8, 512], BF16, "YAt"); r_YAt = Res("YAt")
        YWt = A.alloc([128, 8, 512], BF16, "YWt"); r_YWt = Res("YWt")
        yab = A.alloc([128, 8, 512], BF16, "yab"); r_yab = Res("yab")
        mg = A.alloc([128, 8, 512], BF16, "mg"); r_mg = Res("mg")
        t4 = [(A.alloc([128, 512], F32, "t4"), Res("t4")) for _ in range(2)]
        for i in range(HALF, NT):
            ts_ = slice(i * 512, (i + 1) * 512)
            to_ = slice((i - HALF) * 512, (i - HALF + 1) * 512)
            load_x(nb, i)
            DMA("sp", YAt, YA.rearrange("(k p) t -> p k t", p=128)[:, :, ts_], [], [r_YAt], "YAt")
            DMA("sp", YWt, YW.rearrange("(k p) t -> p k t", p=128)[:, :, ts_], [], [r_YWt], "YWt")
            norm_tile(nb)
            for cb_ in range(8):
                pb, r_pb = nextbank()
                for k in range(8):
                    MM(pb, w3[:, k, 2048 + cb_ * 128:2048 + (cb_ + 1) * 128], hT[:, k, :], k == 0, k == 7,
                       [r_w3, r_hT], [r_pb])
                t, r_t = t4[cb_ % 2]
                ACT(t, pb, AF.Silu, [r_pb], [r_t])
                TT("dve", yab[:, cb_, :], YAt[:, cb_, :], t, ALU.mult, [r_YAt, r_t], [r_yab])
            for eb in range(8):
                gs, r_gs = t4[0]
                ga, r_ga = t4[1]
                for (gt, r_gt, off) in ((gs, r_gs, 0), (ga, r_ga, 1024)):
                    pb, r_pb = nextbank()
                    for k in range(8):
                        MM(pb, w3[:, k, off + eb * 128:off + (eb + 1) * 128], hT[:, k, :], k == 0, k == 7,
                           [r_w3, r_hT], [r_pb])
                    bcol = C_BG + (off // 128) + eb
                    ACT(gt, pb, AF.Sigmoid, [r_pb, r_const], [r_gt], bias=cols[:, bcol:bcol + 1])
                pbs, r_pbs = nextbank()
                for k in range(8):
                    MM(pbs, wp[:, 0, k, eb * 128:(eb + 1) * 128], YWt[:, k, :], k == 0, k == 7, [r_wp, r_YWt], [r_pbs])
                pba, r_pba = nextbank()
                for k in range(8):
                    MM(pba, wp[:, 1, k, eb * 128:(eb + 1) * 128], yab[:, k, :], k == 0, k == 7, [r_wp, r_yab], [r_pba])
                TT("dve", gs, pbs, gs, ALU.mult, [r_pbs, r_gs], [r_gs])
                TT("dve", ga, pba, ga, ALU.mult, [r_pba, r_ga], [r_ga])
                TT("pool", mg[:, eb, :], gs, ga, ALU.add, [r_gs, r_ga], [r_mg])
            for eb in range(8):
                pb, r_pb = nextbank()
                for k in range(8):
                    MM(pb, wp[:, 2, k, eb * 128:(eb + 1) * 128], mg[:, k, :], k == 0, k == 7, [r_wp, r_mg], [r_pb])
                STT("dve", xin[:, eb, :], pb, adac[:, 16 + eb:17 + eb], xin[:, eb, :], ALU.mult, ALU.add,
                    [r_pb, r_ada, r_xin], [r_xin])
            rms_rstd(nb, xin, r_xin)
            for eb in range(8):
                STT("dve", xin[:, eb, :], xin[:, eb, :], cols[:, C_FNW + eb:C_FNW + eb + 1],
                    nb["rstd"], ALU.mult, ALU.mult, [r_xin, r_const, nb["r_rstd"]], [r_xin])
            DMA("pool", outT.rearrange("(k p) t -> p k t", p=128)[:, :, to_], xin, [r_xin], [], "outT")
        final = ["outT"]
    S.emit(final_waits=final)
    return nc

_O = {"z_ssm": (0, 1024), "xbc": (1024, 2304), "dt": (2304, 2320), "q": (2320, 3344), "k": (3344, 4368),
      "v": (4368, 5392), "z_att": (5392, 6416), "f": (6416, 6424), "g": (6424, 8472)}


def _colmajor(v, nblk):
    return np.ascontiguousarray(np.asarray(v, np.float32).reshape(nblk, 128).T)


def make_inputs(b, t, x, c, w_ada, b_ada, norm_w, w_in, conv_w, conv_b, dt_bias, a_log, d_skip,
                ssm_norm_w, b_f, b_gate, w_proj_ssm, w_proj_att, w_out, final_norm_w):
    f32 = np.float32
    SH = x.shape[1] // 2
    w_in = np.asarray(w_in, f32)

    def cs(*names):
        return np.ascontiguousarray(np.concatenate([w_in[:, _O[n][0]:_O[n][1]] for n in names], axis=1))

    cols = np.zeros((128, NCOL), f32)
    cols[:, C_C:C_C + 8] = _colmajor(c[b], 8)
    cols[:, C_BADA:C_BADA + 24] = _colmajor(b_ada, 24)
    cols[:, C_NW:C_NW + 8] = _colmajor(norm_w, 8)
    cw = np.asarray(conv_w, f32).reshape(4, 10, 128).transpose(2, 1, 0).reshape(128, 40)
    cols[:, C_CW:C_CW + 40] = cw
    cols[:, C_CB:C_CB + 10] = _colmajor(conv_b, 10)
    cols[:, C_BG:C_BG + 16] = _colmajor(b_gate, 16)
    cols[:, C_FNW:C_FNW + 8] = _colmajor(final_norm_w, 8)
    cols[:64, C_GM] = 1.0
    cols[:, C_VAL] = 1.0 if t == 1 else 0.0
    cols[:, C_PEN] = 0.0 if t == 1 else 30000.0
    cols[64:, C_GM + 1] = 1.0
    row = np.concatenate([np.asarray(dt_bias, f32), np.asarray(a_log, f32), np.asarray(b_f, f32),
                          np.repeat(np.asarray(d_skip, f32), 64), np.asarray(ssm_norm_w, f32)])
    rows = np.ascontiguousarray(np.broadcast_to(row[None, :], (128, NROW)))
    j = np.arange(128)
    U = (j[:, None] <= j[None, :]).astype(f32)
    SL = (j[:, None] > j[None, :]).astype(f32)
    half = np.broadcast_to((j[:, None] <= 63), (128, 128)).astype(f32)
    masks = np.ascontiguousarray(np.concatenate([U, SL, half, np.ones((128, 128), f32), np.eye(128, dtype=f32)], axis=1))
    return {
        "xT": np.ascontiguousarray(np.concatenate([np.asarray(x[b, :SH], f32), np.asarray(x[b, t * SH:(t + 1) * SH], f32)], axis=0).T),
        "wA1": cs("q", "k", "v", "f"), "wA2": cs("xbc", "z_ssm", "dt"), "wC": cs("g", "z_att"),
        "wada": np.ascontiguousarray(np.asarray(w_ada, f32)),
        "wps": np.ascontiguousarray(np.asarray(w_proj_ssm, f32)),
        "wpa": np.ascontiguousarray(np.asarray(w_proj_att, f32)),
        "wout": np.ascontiguousarray(np.asarray(w_out, f32)),
        "cols": cols, "rows": rows, "masks": masks,
    }


_NC_CACHE = {}


def kernel(**inputs):
    inputs = {k: np.asarray(v) for k, v in inputs.items()}
    x = inputs["x"]
    B, S_TOK, D = x.shape
    if S_TOK not in _NC_CACHE:
        _NC_CACHE[S_TOK] = build(S_TOK)
    nc = _NC_CACHE[S_TOK]
    in_maps = [make_inputs(i // 2, i % 2, **inputs) for i in range(8)]
    res = run_bass_kernel_spmd(nc, in_maps, core_ids=list(range(8)))
    out = np.empty((B, S_TOK, D), np.float32)
    half = S_TOK // 2
    for b in range(B):
        out[b, :half] = res.results[2 * b]["outT"].T
        out[b, half:] = res.results[2 * b + 1]["outT"].T
    return out
```
